# Optimizing a Trainium2 kernel written in Bass

```python
import jax, jax.numpy as jnp
from jax import lax
import numpy as np


D_MODEL = 1024
BATCH = 8
SEQ = 4096
DEPTH = 4

N_MIXERS = 3
POOL_WINDOWS = (2, 4, 8, 16)
POOL_GROUPS = len(POOL_WINDOWS)
POOL_GROUP_DIM = D_MODEL // POOL_GROUPS
SGU_CHUNK = 128
SGU_WIDTH = D_MODEL
SGU_HEAD_DIM = 128
SGU_HEADS = SGU_WIDTH // SGU_HEAD_DIM
MLA_HEADS = 16
MLA_Q_LORA = 256
MLA_KV_LORA = 128
MLA_NOPE = 128
MLA_ROPE = 64
MLA_V = 128
ROPE_THETA = 10000.0
Q_BLOCK = 128
D_FF = 4 * D_MODEL
RMS_EPS = 1e-6
LN_EPS = 1e-5
MAX_POS_OFFSET = 4096
N_POOL_LAYERS = len(range(0, DEPTH, N_MIXERS))
N_SGU_LAYERS = len(range(1, DEPTH, N_MIXERS))
N_MLA_LAYERS = len(range(2, DEPTH, N_MIXERS))

kernel_name = 'hybrid_pool_sgu_mla_decoder'


def rmsnorm(x, g):
    xf = x.astype(jnp.float32)
    y = xf * lax.rsqrt(jnp.mean(xf * xf, axis=-1, keepdims=True) + RMS_EPS)
    return (y * g.astype(jnp.float32)).astype(x.dtype)


def layernorm(x, g, b):
    xf = x.astype(jnp.float32)
    mu = jnp.mean(xf, axis=-1, keepdims=True)
    xc = xf - mu
    var = jnp.mean(xc * xc, axis=-1, keepdims=True)
    y = xc * lax.rsqrt(var + LN_EPS)
    return (y * g.astype(jnp.float32) + b.astype(jnp.float32)).astype(x.dtype)


def modulate(h, shift, scale):
    return h * (1.0 + scale[:, None, :]) + shift[:, None, :]


def pool_mixer(h, w, scale):
    B_, S_, _ = h.shape
    hf = h.astype(jnp.float32).reshape(B_, S_, POOL_GROUPS, POOL_GROUP_DIM)
    cs = jnp.cumsum(hf, axis=1)
    cs = jnp.concatenate([jnp.zeros_like(cs[:, :1]), cs], axis=1)
    t = jnp.arange(S_, dtype=jnp.float32)
    outs = []
    for gi, win in enumerate(POOL_WINDOWS):
        csg = cs[:, :, gi]
        upper = csg[:, 1:]
        lower = jnp.concatenate([jnp.zeros_like(csg[:, :win - 1]), csg[:, :S_ - win + 1]], axis=1)
        count = jnp.minimum(t + 1.0, float(win))[None, :, None]
        outs.append((upper - lower) / count - hf[:, :, gi])
    pooled = jnp.stack(outs, axis=2).astype(h.dtype)
    y = jnp.einsum('bsgc,gcd->bsgd', pooled, w).reshape(B_, S_, D_MODEL)
    return y * scale


def sgu_mixer(h, w_in, ln_g, ln_b, w_s, b_s, w_out):
    B_, S_, _ = h.shape
    z = jax.nn.gelu(h @ w_in, approximate=False)
    u, v = jnp.split(z, 2, axis=-1)
    v = layernorm(v, ln_g, ln_b)
    nc = S_ // SGU_CHUNK
    v = v.reshape(B_, nc, SGU_CHUNK, SGU_HEADS, SGU_HEAD_DIM)
    mask = jnp.tril(jnp.ones((SGU_CHUNK, SGU_CHUNK), dtype=bool))
    ws = jnp.where(mask[None], w_s, 0)
    mixed = jnp.einsum('hts,bnshc->bnthc', ws, v) + b_s.T[None, None, :, :, None]
    gated = u * mixed.reshape(B_, S_, SGU_WIDTH)
    return gated @ w_out


def apply_rope(x, cos, sin):
    x1, x2 = jnp.split(x, 2, axis=-1)
    return jnp.concatenate([x1 * cos - x2 * sin, x2 * cos + x1 * sin], axis=-1)


def mla_mixer(h, positions, w_dq_dkv, q_norm_g, kv_norm_g, w_uq, w_ukv, w_o):
    B_, S_, _ = h.shape
    lat = h @ w_dq_dkv
    c_q, c_kv, k_rope = jnp.split(lat, [MLA_Q_LORA, MLA_Q_LORA + MLA_KV_LORA], axis=-1)
    c_q = rmsnorm(c_q, q_norm_g)
    c_kv = rmsnorm(c_kv, kv_norm_g)
    q = (c_q @ w_uq).reshape(B_, S_, MLA_HEADS, MLA_NOPE + MLA_ROPE)
    q_nope, q_rope = jnp.split(q, [MLA_NOPE], axis=-1)
    kv = (c_kv @ w_ukv).reshape(B_, S_, MLA_HEADS, MLA_NOPE + MLA_V)
    k_nope, v = jnp.split(kv, [MLA_NOPE], axis=-1)
    inv_freq = ROPE_THETA ** (-jnp.arange(0, MLA_ROPE, 2, dtype=jnp.float32) / MLA_ROPE)
    ang = positions.astype(jnp.float32)[..., None] * inv_freq
    cos, sin = jnp.cos(ang), jnp.sin(ang)
    q_rope = apply_rope(q_rope.astype(jnp.float32), cos[:, :, None], sin[:, :, None]).astype(h.dtype)
    k_rope = apply_rope(k_rope.astype(jnp.float32), cos, sin).astype(h.dtype)
    sm_scale = (MLA_NOPE + MLA_ROPE) ** -0.5
    nb = S_ // Q_BLOCK

    def to_blocks(t):
        return jnp.moveaxis(t.reshape(B_, nb, Q_BLOCK, *t.shape[2:]), 1, 0)

    k_idx = jnp.arange(S_)

    def attend_block(args):
        qn, qr, blk = args
        s = jnp.einsum('bqhd,bkhd->bhqk', qn, k_nope, preferred_element_type=jnp.float32)
        s = s + jnp.einsum('bqhr,bkr->bhqk', qr, k_rope, preferred_element_type=jnp.float32)
        q_idx = blk * Q_BLOCK + jnp.arange(Q_BLOCK)
        causal = k_idx[None, :] <= q_idx[:, None]
        s = jnp.where(causal[None, None], s * sm_scale, -1e30)
        p = jax.nn.softmax(s, axis=-1).astype(v.dtype)
        return jnp.einsum('bhqk,bkhd->bqhd', p, v)

    out = lax.map(attend_block, (to_blocks(q_nope), to_blocks(q_rope), jnp.arange(nb)))
    out = jnp.moveaxis(out, 0, 1).reshape(B_, S_, MLA_HEADS * MLA_V)
    return out @ w_o


def sq_relu_mlp(h, w1, w2):
    return jnp.square(jax.nn.relu(h @ w1)) @ w2


def setup_inputs(seed: int = 0) -> dict:
    key = jax.random.key(seed)
    ks = jax.random.split(key, 32)
    f32 = jnp.float32

    def nrm(k, shape, std):
        return jax.random.normal(k, shape, f32) * std

    def gain(k, shape):
        return 1.0 + 0.1 * jax.random.normal(k, shape, f32)

    x = jax.random.normal(ks[0], (BATCH, SEQ, D_MODEL), f32)
    c = jax.random.normal(ks[1], (BATCH, D_MODEL), f32)
    offset = jax.random.randint(ks[2], (BATCH, 1), 0, MAX_POS_OFFSET, dtype=jnp.int32)
    positions = (offset + jnp.arange(SEQ, dtype=jnp.int32)[None, :]).astype(jnp.int32)
    return {
        'x': x,
        'c': c,
        'positions': positions,
        'ada_w': nrm(ks[3], (DEPTH, D_MODEL, 6 * D_MODEL), 0.5 * D_MODEL ** -0.5),
        'ada_b': nrm(ks[4], (DEPTH, 6 * D_MODEL), 0.02),
        'norm_mix_g': gain(ks[5], (DEPTH, D_MODEL)),
        'norm_mlp_g': gain(ks[6], (DEPTH, D_MODEL)),
        'pool_w': nrm(ks[7], (N_POOL_LAYERS, POOL_GROUPS, POOL_GROUP_DIM, POOL_GROUP_DIM), POOL_GROUP_DIM ** -0.5),
        'pool_scale': gain(ks[8], (N_POOL_LAYERS, D_MODEL)),
        'sgu_w_in': nrm(ks[9], (N_SGU_LAYERS, D_MODEL, 2 * SGU_WIDTH), D_MODEL ** -0.5),
        'sgu_ln_g': gain(ks[10], (N_SGU_LAYERS, SGU_WIDTH)),
        'sgu_ln_b': nrm(ks[11], (N_SGU_LAYERS, SGU_WIDTH), 0.02),
        'sgu_w_s': nrm(ks[12], (N_SGU_LAYERS, SGU_HEADS, SGU_CHUNK, SGU_CHUNK), SGU_CHUNK ** -0.5),
        'sgu_b_s': gain(ks[13], (N_SGU_LAYERS, SGU_HEADS, SGU_CHUNK)),
        'sgu_w_out': nrm(ks[14], (N_SGU_LAYERS, SGU_WIDTH, D_MODEL), SGU_WIDTH ** -0.5),
        'mla_w_dq_dkv': nrm(ks[15], (N_MLA_LAYERS, D_MODEL, MLA_Q_LORA + MLA_KV_LORA + MLA_ROPE), D_MODEL ** -0.5),
        'mla_q_norm_g': gain(ks[16], (N_MLA_LAYERS, MLA_Q_LORA)),
        'mla_kv_norm_g': gain(ks[17], (N_MLA_LAYERS, MLA_KV_LORA)),
        'mla_w_uq': nrm(ks[18], (N_MLA_LAYERS, MLA_Q_LORA, MLA_HEADS * (MLA_NOPE + MLA_ROPE)), MLA_Q_LORA ** -0.5),
        'mla_w_ukv': nrm(ks[19], (N_MLA_LAYERS, MLA_KV_LORA, MLA_HEADS * (MLA_NOPE + MLA_V)), MLA_KV_LORA ** -0.5),
        'mla_w_o': nrm(ks[20], (N_MLA_LAYERS, MLA_HEADS * MLA_V, D_MODEL), (MLA_HEADS * MLA_V) ** -0.5),
        'mlp_w1': nrm(ks[21], (DEPTH, D_MODEL, D_FF), D_MODEL ** -0.5),
        'mlp_w2': nrm(ks[22], (DEPTH, D_FF, D_MODEL), D_FF ** -0.5),
        'final_g': gain(ks[23], (D_MODEL,)),
    }


def reference(x, c, positions, ada_w, ada_b, norm_mix_g, norm_mlp_g, pool_w, pool_scale,
              sgu_w_in, sgu_ln_g, sgu_ln_b, sgu_w_s, sgu_b_s, sgu_w_out,
              mla_w_dq_dkv, mla_q_norm_g, mla_kv_norm_g, mla_w_uq, mla_w_ukv, mla_w_o,
              mlp_w1, mlp_w2, final_g):
    c_act = jax.nn.silu(c)
    for i in range(DEPTH):
        mod = c_act @ ada_w[i] + ada_b[i]
        sh1, sc1, g1, sh2, sc2, g2 = jnp.split(mod, 6, axis=-1)
        h = modulate(rmsnorm(x, norm_mix_g[i]), sh1, sc1)
        kind, j = i % N_MIXERS, i // N_MIXERS
        if kind == 0:
            y = pool_mixer(h, pool_w[j], pool_scale[j])
        elif kind == 1:
            y = sgu_mixer(h, sgu_w_in[j], sgu_ln_g[j], sgu_ln_b[j], sgu_w_s[j], sgu_b_s[j], sgu_w_out[j])
        else:
            y = mla_mixer(h, positions, mla_w_dq_dkv[j], mla_q_norm_g[j], mla_kv_norm_g[j],
                          mla_w_uq[j], mla_w_ukv[j], mla_w_o[j])
        x = x + g1[:, None, :] * y
        h = modulate(rmsnorm(x, norm_mlp_g[i]), sh2, sc2)
        x = x + g2[:, None, :] * sq_relu_mlp(h, mlp_w1[i], mlp_w2[i])
    return rmsnorm(x, final_g)
```

```python
import numpy as np
import os as _os
from contextlib import ExitStack
import concourse.bass as bass
import concourse.mybir as mybir
from concourse.bass_utils import run_bass_kernel_spmd

F32 = mybir.dt.float32
BF16 = mybir.dt.bfloat16
I32 = mybir.dt.int32
AF = mybir.ActivationFunctionType
ALU = mybir.AluOpType
AX = mybir.AxisListType

D = 1024
S = 4096
NCH = 8
DFF = 4096
DEPTH = 4
RMS_EPS = 1e-6
LN_EPS = 1e-5
NEG = -30000.0

V_LAYER = 64
V_POOLSC = 256
V_FINALG = 272
V_QNG = 280
V_KVNG = 282
V_C = 283
V_FREQ = 291
V_INVC = 293
V_EPS = 357
V_HALFPI = 359
V_LNG = 360
V_LNB = 368
NV = 376


class Res:
    __slots__ = ("name", "w", "wg", "r", "rd", "sem", "cnt")

    def __init__(self, name):
        self.name = name
        self.w = None
        self.wg = []
        self.r = {}
        self.rd = []
        self.sem = None
        self.cnt = 0


class Op:
    __slots__ = ("eng", "fn", "deps", "signal", "sem", "token", "dma", "wres", "idx")

    def __init__(self, eng, fn, dma):
        self.idx = 0
        self.eng = eng
        self.fn = fn
        self.dma = dma
        self.deps = []
        self.signal = False
        self.sem = None
        self.token = 0
        self.wres = None


class Prog:
    def __init__(self, nc, stack):
        self.nc = nc
        self.stack = stack
        self.ops = []
        self.engs = {"pe": nc.tensor, "act": nc.scalar, "dve": nc.vector,
                     "pool": nc.gpsimd, "sp": nc.sync}
        self.nsem = 0
        self.eidx = {k: 0 for k in self.engs}
        self.log = []
        self.fence_ops = []

    def newsem(self, name):
        self.nsem += 1
        return self.stack.enter_context(self.nc.semaphore(f"{name}_{self.nsem}"))

    def op(self, eng, fn, R=(), W=(), dma=False, group=False):
        o = Op(eng, fn, dma)
        self.eidx[eng] += 1
        o.idx = self.eidx[eng]
        deps = set()
        raw = set()
        for r in R:
            if r.w is not None:
                deps.add(r.w)
                raw.add(r.w)
            for g in r.wg:
                deps.add(g)
        for w in W:
            if w.w is not None:
                deps.add(w.w)
            if not group:
                for g in w.wg:
                    deps.add(g)
            for ro in w.r.values():
                deps.add(ro)
            for ro in w.rd:
                deps.add(ro)
        for f in self.fence_ops:
            deps.add(f)
        o.deps = [d for d in deps if d.dma or d.eng != eng or eng != "pe"]
        for d in o.deps:
            d.signal = True
        for r in R:
            if dma:
                r.rd.append(o)
            else:
                r.r[eng] = o
        for w in W:
            if group:
                w.wg.append(o)
            else:
                w.w = o
                w.wg = []
            w.r = {}
            w.rd = []
        if dma:
            o.wres = W[0]
        self.ops.append(o)
        return o

    def fence(self, res_list):
        ops = set()
        for r in res_list:
            if r.w is not None:
                ops.add(r.w)
            ops.update(r.wg)
            ops.update(r.r.values())
            ops.update(r.rd)
        self.fence_ops = list(ops)

    def emit(self):
        esem = {k: self.newsem("e_" + k) for k in self.engs}
        ecnt = {k: 0 for k in self.engs}
        seen = {k: {} for k in self.engs}
        nwait = 0
        for o in self.ops:
            e = self.engs[o.eng]
            sn = seen[o.eng]
            for d in o.deps:
                key = id(d.sem)
                if sn.get(key, 0) < d.token:
                    e.wait_ge(d.sem, d.token)
                    sn[key] = d.token
                    nwait += 1
            ins = o.fn(e)
            if _os.environ.get("DUMPOPS"):
                self.log.append((o.eng, o.idx, o.dma, [(d.eng, d.idx, d.token) for d in o.deps], o.signal, str(ins)[:150]))
            if o.dma:
                r = o.wres
                if r.sem is None:
                    r.sem = self.newsem("d_" + r.name)
                r.cnt += 16
                ins.then_inc(r.sem, 16)
                o.sem = r.sem
                o.token = r.cnt
            elif o.signal:
                ecnt[o.eng] += 1
                ins.then_inc(esem[o.eng], 1)
                o.sem = esem[o.eng]
                o.token = ecnt[o.eng]
        return nwait

    def mm(self, out, lhsT, rhs, start, stop, R, W):
        return self.op("pe", lambda e: e.matmul(out, lhsT, rhs, start=start, stop=stop), R, W)

    def act(self, out, in_, func, R, W, bias=None, scale=None, accum_out=None):
        kw = {}
        if bias is not None:
            kw["bias"] = bias
        if scale is not None:
            kw["scale"] = scale
        if accum_out is not None:
            kw["accum_out"] = accum_out
        return self.op("act", lambda e: e.activation(out, in_, func, **kw), R, W)

    def tt(self, eng, out, in0, in1, op, R, W):
        return self.op(eng, lambda e: e.tensor_tensor(out, in0, in1, op), R, W)

    def ts(self, eng, out, in0, s1, s2, op0, op1, R, W):
        if s2 is None:
            return self.op(eng, lambda e: e.tensor_single_scalar(out, in0, s1, op0), R, W)
        return self.op(eng, lambda e: e.tensor_scalar(out, in0, s1, s2, op0, op1), R, W)

    def stt(self, eng, out, in0, scalar, in1, op0, op1, R, W):
        return self.op(eng, lambda e: e.scalar_tensor_tensor(out, in0, scalar, in1, op0, op1), R, W)

    def copy(self, eng, out, in_, R, W):
        return self.op(eng, lambda e: e.tensor_copy(out, in_), R, W)

    def memset(self, eng, ap, val, W):
        return self.op(eng, lambda e: e.memset(ap, val), (), W)

    def recip(self, out, in_, R, W):
        return self.op("dve", lambda e: e.reciprocal(out, in_), R, W)

    def dma(self, eng, out, in_, R, W, group=False):
        return self.op(eng, lambda e: e.dma_start(out=out, in_=in_), R, W, dma=True, group=group)


class Arena:
    def __init__(self, ap, nwords):
        self.ap = ap
        self.n = nwords
        self.off = 0
        self.marks = []
        self.peak = 0

    def push(self):
        self.marks.append(self.off)

    def pop(self):
        self.off = self.marks.pop()

    def tile(self, free, dt, parts=128):
        sz = 4 if dt in (F32, I32) else 2
        n = 1
        for f in free:
            n *= f
        words = (n * sz + 3) // 4
        words = (words + 7) // 8 * 8
        assert self.off + words <= self.n, f"arena overflow {self.off}+{words}>{self.n}"
        a = self.ap[:, self.off:self.off + words]
        self.off += words
        self.peak = max(self.peak, self.off)
        if dt != F32:
            a = a.bitcast(dt)
        a = a[:, 0:n]
        if len(free) == 2:
            a = a.rearrange("p (a b) -> p a b", a=free[0])
        elif len(free) == 3:
            a = a.rearrange("p (a b c) -> p a b c", a=free[0], b=free[1])
        if parts != 128:
            a = a[0:parts]
        return a


ARENA_WORDS = (212000 - 131072) // 4 - 64


def build(n_layers=DEPTH, final=True):
    nc = bass.Bass("TRN2", target_bir_lowering=False)
    stack = ExitStack()
    P = Prog(nc, stack)

    def dram_in(name, shape, dt=F32):
        return nc.dram_tensor(name, list(shape), dt, kind="ExternalInput").ap()

    def dram_tmp(name, shape, dt=BF16):
        return nc.dram_tensor(name, list(shape), dt, kind="Internal").ap()

    xT_d = dram_in("xT", [D, S])
    vecs_d = dram_in("vecs", [128, NV])
    cmat_d = dram_in("cmat", [128, 384])
    pos_d = dram_in("pos", [1, S], I32)
    ada_w_d = dram_in("ada_w", [DEPTH, D, 6 * D])
    w1_d = dram_in("mlp_w1", [DEPTH, D, DFF])
    w2_d = dram_in("mlp_w2", [DEPTH, DFF, D])
    poolw_d = dram_in("pool_w", [2, 4, 256, 256])
    sgu_win_d = dram_in("sgu_w_in", [D, 2 * D])
    sgu_wout_d = dram_in("sgu_w_out", [D, D])
    sgu_wsT_d = dram_in("sgu_wsT", [8, 128, 128])
    sgu_ln_d = dram_in("sgu_ln", [2, D])
    sgu_bs_d = dram_in("sgu_bs", [1, D])
    mla_wdq_d = dram_in("mla_wdq", [D, 448])
    mla_wuq_d = dram_in("mla_wuq", [256, 3072])
    mla_wukT_d = dram_in("mla_wukT", [16, 128, 128])
    mla_wuv_d = dram_in("mla_wuv", [128, 16, 128])
    mla_wo_d = dram_in("mla_wo", [2048, D])
    out_d = nc.dram_tensor("outT", [D, S], F32, kind="ExternalOutput").ap()

    w1s = [dram_tmp(f"w1s{i}", [8, 128, 4096]) for i in range(DEPTH)]
    w2s = [dram_tmp(f"w2s{i}", [8, 128, 4096]) for i in range(DEPTH)]
    poolws = [dram_tmp(f"poolws{j}", [128, 2048]) for j in range(2)]
    R_w1s = [Res(f"w1s{i}") for i in range(DEPTH)]
    R_w2s = [Res(f"w2s{i}") for i in range(DEPTH)]
    R_poolws = [Res(f"poolws{j}") for j in range(2)]
    sgu_wins = dram_tmp("sgu_wins", [4, 128, 4096])
    sgu_wouts = dram_tmp("sgu_wouts", [2, 128, 4096])
    R_sgu_wins = Res("sgu_wins")
    R_sgu_wouts = Res("sgu_wouts")
    mla_wdqs = dram_tmp("mla_wdqs", [128, 8 * 448])
    mla_grps = dram_tmp("mla_grps", [4, 128, 2560])
    mla_wos = dram_tmp("mla_wos", [4, 128, 4096])
    R_mla_wdqs = Res("mla_wdqs")
    R_mla_grps = Res("mla_grps")
    R_mla_wos = Res("mla_wos")

    xT = stack.enter_context(nc.sbuf_tensor("xT_sb", [128, NCH, S], F32))
    arena_t = stack.enter_context(nc.sbuf_tensor("arena", [128, ARENA_WORDS], F32))
    A = Arena(arena_t[:, :], ARENA_WORDS)
    psb = [stack.enter_context(nc.psum_tensor(f"ps{i}", [128, 512], F32)) for i in range(8)]
    R_ps = [Res(f"ps{i}") for i in range(8)]
    R_x = [Res(f"x{t}") for t in range(32)]

    def xres(t0, T):
        return R_x[t0 // 128:(t0 + T) // 128]

    vecs = A.tile([NV], F32)
    R_vecs = Res("vecs")
    P.dma("sp", vecs, vecs_d, (), [R_vecs])
    ones_m = A.tile([128], BF16)
    R_c = Res("consts")
    P.memset("dve", ones_m, 1.0 / D, [R_c])
    ones_q = A.tile([128], BF16)
    P.memset("dve", ones_q, 1.0 / 256, [R_c])
    ones_kv = A.tile([128], BF16)
    P.memset("dve", ones_kv, 1.0 / 128, [R_c])
    ones_1 = A.tile([128], BF16)
    P.memset("dve", ones_1, 1.0, [R_c])
    cmat = A.tile([384], F32)
    R_cmat = Res("cmat")
    P.dma("sp", cmat, cmat_d, (), [R_cmat])
    ident = A.tile([128], BF16)
    P.copy("dve", ident, cmat[:, 0:128], [R_cmat], [R_c])
    modv = A.tile([DEPTH, 48], F32)
    R_mod = Res("mod")
    eps_rms = vecs[:, V_EPS:V_EPS + 1]
    eps_ln = vecs[:, V_EPS + 1:V_EPS + 2]

    L = {}

    def begin_layer(T, nslot):
        A.push()
        L["ring"] = [A.tile([4096], BF16) for _ in range(nslot)]
        L["R_ring"] = [Res(f"ring{i}") for i in range(nslot)]
        L["ring_i"] = 0
        L["sq"] = [A.tile([T], BF16) for _ in range(2)]
        L["R_sq"] = [Res("sq0"), Res("sq1")]
        L["rt"] = A.tile([T], F32)
        L["R_rt"] = Res("rt")
        L["rstd"] = A.tile([T], F32)
        L["R_rstd"] = Res("rstd")
        L["tmp"] = [A.tile([T], F32) for _ in range(2)]
        L["R_tmp"] = [Res("tmp0"), Res("tmp1")]

    def end_layer(extra):
        P.fence(list(extra) + L["R_ring"] + L["R_sq"] + [L["R_rt"], L["R_rstd"]] + L["R_tmp"])
        A.pop()

    def wload(src_ap, src_res):
        n_ = len(L["ring"])
        i = L["ring_i"] % n_
        L["ring_i"] += 1
        n = 1
        for s_ in src_ap.shape[1:]:
            n *= s_
        P.dma("sp", L["ring"][i][:, 0:n], src_ap, [src_res], [L["R_ring"][i]])
        return L["ring"][i], L["R_ring"][i]

    xv = xT_d.rearrange("(c p) t -> p c t", p=128)
    for t in range(8):
        P.dma("sp", xT[:, :, t * 512:(t + 1) * 512], xv[:, :, t * 512:(t + 1) * 512], (),
              R_x[4 * t:4 * t + 4])

    def cast_mlp(i):
        for q in range(8):
            src = w1_d[i][:, q * 512:(q + 1) * 512].rearrange("(kc p) c -> p kc c", p=128)
            dst = w1s[i][q].rearrange("p (kc c) -> p kc c", kc=8)
            P.dma("pool", dst, src, (), [R_w1s[i]], group=True)
        for q in range(8):
            src = w2_d[i][q * 512:(q + 1) * 512, :].rearrange("(fc p) d -> p fc d", p=128)
            dst = w2s[i][q].rearrange("p (fc d) -> p fc d", fc=4)
            P.dma("pool", dst, src, (), [R_w2s[i]], group=True)

    def cast_pool(j):
        src = poolw_d[j].rearrange("g (kc p) d -> p g kc d", p=128)
        dst = poolws[j].rearrange("p (g kc d) -> p g kc d", g=4, kc=2)
        P.dma("pool", dst, src, (), [R_poolws[j]], group=True)

    def cast_sgu():
        for q in range(4):
            src = sgu_win_d[:, q * 512:(q + 1) * 512].rearrange("(kc p) c -> p kc c", p=128)
            dst = sgu_wins[q].rearrange("p (kc c) -> p kc c", kc=8)
            P.dma("pool", dst, src, (), [R_sgu_wins], group=True)
        for q in range(2):
            src = sgu_wout_d[:, q * 512:(q + 1) * 512].rearrange("(kc p) c -> p kc c", p=128)
            dst = sgu_wouts[q].rearrange("p (kc c) -> p kc c", kc=8)
            P.dma("pool", dst, src, (), [R_sgu_wouts], group=True)

    def cast_mla():
        src = mla_wdq_d.rearrange("(kc p) c -> p kc c", p=128)
        dst = mla_wdqs.rearrange("p (kc c) -> p kc c", kc=8)
        P.dma("pool", dst, src, (), [R_mla_wdqs], group=True)
        for g in range(4):
            src = mla_wuq_d[:, g * 768:(g + 1) * 768].rearrange("(kc p) c -> p kc c", p=128)
            dst = mla_grps[g][:, 0:1536].rearrange("p (kc c) -> p kc c", kc=2)
            P.dma("pool", dst, src, (), [R_mla_grps], group=True)
            src = mla_wukT_d[4 * g:4 * g + 4].rearrange("h n c -> n h c")
            dst = mla_grps[g][:, 1536:2048].rearrange("p (h c) -> p h c", h=4)
            P.dma("pool", dst, src, (), [R_mla_grps], group=True)
            src = mla_wuv_d[:, 4 * g:4 * g + 4, :]
            dst = mla_grps[g][:, 2048:2560].rearrange("p (h c) -> p h c", h=4)
            P.dma("pool", dst, src, (), [R_mla_grps], group=True)
            src = mla_wo_d[g * 512:(g + 1) * 512, :].rearrange("(h p) d -> p h d", p=128)
            dst = mla_wos[g].rearrange("p (h d) -> p h d", h=4)
            P.dma("pool", dst, src, (), [R_mla_wos], group=True)

    A.push()
    cact32 = A.tile([8], F32)
    cact = A.tile([8], BF16)
    R_cact = Res("cact")
    P.act(cact32, vecs[:, V_C:V_C + 8], AF.Silu, [R_vecs], [R_cact])
    P.copy("dve", cact, cact32, [R_cact], [R_cact])
    ada_slots = [A.tile([8, 512], BF16) for _ in range(2)]
    R_ada = [Res(f"ada{i}") for i in range(2)]
    cast_pool(0)
    k = 0
    for i in range(n_layers):
        av = ada_w_d[i].rearrange("(kc p) f -> p kc f", p=128)
        for pc in range(12):
            sl, rs = ada_slots[k % 2], R_ada[k % 2]
            k += 1
            P.dma("pool", sl, av[:, :, pc * 512:(pc + 1) * 512], (), [rs])
            for jj in range(4):
                j = pc * 4 + jj
                for kc in range(8):
                    P.mm(psb[0][:, i * 48 + j:i * 48 + j + 1], sl[:, kc, jj * 128:(jj + 1) * 128],
                         cact[:, kc:kc + 1], kc == 0, kc == 7, [rs, R_cact], [R_ps[0]])
        if i == 0:
            cast_mlp(0)
    for i in range(n_layers):
        P.tt("dve", modv[:, i, :], psb[0][:, i * 48:(i + 1) * 48],
             vecs[:, i * V_LAYER:i * V_LAYER + 48], ALU.add, [R_ps[0], R_vecs], [R_mod])
    P.fence([R_cact, R_ada[0], R_ada[1]])
    A.pop()
    if n_layers > 1:
        cast_sgu()
        cast_mlp(1)
    if n_layers > 2:
        cast_mla()
        cast_mlp(2)
    if n_layers > 3:
        cast_pool(1)
        cast_mlp(3)

    lv = A.tile([DEPTH, 6, 8], F32)
    R_lv = Res("lv")

    def derive(i):
        kind, j = i % 3, i // 3
        m = modv[:, i, :]
        vb = i * V_LAYER
        P.stt("dve", lv[:, i, 0, :], m[:, 8:16], 1.0, vecs[:, vb + 48:vb + 56], ALU.add, ALU.mult,
              [R_mod, R_vecs], [R_lv])
        P.copy("dve", lv[:, i, 1, :], m[:, 0:8], [R_mod], [R_lv])
        if kind == 0:
            P.tt("dve", lv[:, i, 2, :], m[:, 16:24], vecs[:, V_POOLSC + 8 * j:V_POOLSC + 8 * j + 8],
                 ALU.mult, [R_mod, R_vecs], [R_lv])
        else:
            P.copy("dve", lv[:, i, 2, :], m[:, 16:24], [R_mod], [R_lv])
        P.stt("dve", lv[:, i, 3, :], m[:, 32:40], 1.0, vecs[:, vb + 56:vb + 64], ALU.add, ALU.mult,
              [R_mod, R_vecs], [R_lv])
        P.copy("dve", lv[:, i, 4, :], m[:, 24:32], [R_mod], [R_lv])
        P.copy("dve", lv[:, i, 5, :], m[:, 40:48], [R_mod], [R_lv])

    cnt = {"sq": 0, "tmp": 0, "up": 0, "dn": 0}

    def rms_rstd(t0, T):
        sq, R_sq, rt, R_rt, rstd, R_rstd = L["sq"], L["R_sq"], L["rt"], L["R_rt"], L["rstd"], L["R_rstd"]
        for c in range(NCH):
            b = cnt["sq"] % 2
            cnt["sq"] += 1
            P.act(sq[b][:, 0:T], xT[:, c, t0:t0 + T], AF.Square, xres(t0, T), [R_sq[b]])
            P.mm(psb[0][:, 0:T], ones_m, sq[b][:, 0:T], c == 0, c == NCH - 1,
                 [R_sq[b], R_c], [R_ps[0]])
        P.act(rt[:, 0:T], psb[0][:, 0:T], AF.Sqrt, [R_ps[0], R_vecs], [R_rt], bias=eps_rms, scale=1.0)
        P.recip(rstd[:, 0:T], rt[:, 0:T], [R_rt], [R_rstd])

    def norm_mod(t0, T, Acol, Bcol, dst, dst_res, dst_off=0):
        rms_rstd(t0, T)
        tmp, R_tmp, rstd, R_rstd = L["tmp"], L["R_tmp"], L["rstd"], L["R_rstd"]
        for c in range(NCH):
            b = cnt["tmp"] % 2
            cnt["tmp"] += 1
            P.tt("dve", tmp[b][:, 0:T], xT[:, c, t0:t0 + T], rstd[:, 0:T], ALU.mult,
                 xres(t0, T) + [R_rstd], [R_tmp[b]])
            P.act(dst[:, c, dst_off:dst_off + T], tmp[b][:, 0:T], AF.Identity,
                  [R_tmp[b], R_lv], [dst_res], bias=Bcol[:, c:c + 1], scale=Acol[:, c:c + 1])

    UP_BANKS = [1, 2, 3]
    DN_BANKS = [4, 5, 6, 7]

    def up_bank():
        b = UP_BANKS[cnt["up"] % len(UP_BANKS)]
        cnt["up"] += 1
        return b

    def dn_bank():
        b = DN_BANKS[cnt["dn"] % len(DN_BANKS)]
        cnt["dn"] += 1
        return b

    def resid_add(t0, T, d, bank, Gcol):
        P.stt("dve", xT[:, d, t0:t0 + T], psb[bank][:, 0:T], Gcol[:, d:d + 1], xT[:, d, t0:t0 + T],
              ALU.mult, ALU.add, [R_ps[bank], R_lv] + xres(t0, T), xres(t0, T))

    def mlp_layer(i):
        T = 512
        begin_layer(T, 4)
        hT = [A.tile([8, T], BF16) for _ in range(2)]
        R_h = [Res("h0"), Res("h1")]
        aT = [A.tile([4, T], BF16) for _ in range(2)]
        R_a = [Res("a0"), Res("a1")]
        rl = [A.tile([T], F32) for _ in range(2)]
        R_rl = [Res("rl0"), Res("rl1")]
        nrl = 0
        for t in range(S // T):
            t0 = t * T
            h, rh = hT[t % 2], R_h[t % 2]
            norm_mod(t0, T, lv[:, i, 3, :], lv[:, i, 4, :], h, rh)
            for q in range(8):
                w1, rw1 = wload(w1s[i][q], R_w1s[i])
                w1 = w1.rearrange("p (kc c) -> p kc c", kc=8)
                w2, rw2 = wload(w2s[i][q], R_w2s[i])
                w2 = w2.rearrange("p (fc d) -> p fc d", fc=4)
                a, ra = aT[q % 2], R_a[q % 2]
                for f in range(4):
                    bk = up_bank()
                    for kc in range(8):
                        P.mm(psb[bk][:, 0:T], w1[:, kc, f * 128:(f + 1) * 128], h[:, kc, :],
                             kc == 0, kc == 7, [rw1, rh], [R_ps[bk]])
                    r_, rr = rl[nrl % 2], R_rl[nrl % 2]
                    nrl += 1
                    P.act(r_, psb[bk][:, 0:T], AF.Relu, [R_ps[bk]], [rr])
                    P.tt("pool", a[:, f, :], r_, r_, ALU.mult, [rr], [ra])
                for d in range(8):
                    bk = dn_bank()
                    for f in range(4):
                        P.mm(psb[bk][:, 0:T], w2[:, f, d * 128:(d + 1) * 128], a[:, f, :],
                             f == 0, f == 3, [rw2, ra], [R_ps[bk]])
                    resid_add(t0, T, d, bk, lv[:, i, 5, :])
        end_layer(R_h + R_a + R_rl)

    def pool_layer(i):
        j = i // 3
        T = 256
        H = 16
        begin_layer(T, 0)
        hp = A.tile([8, H + T], F32)
        R_hp = Res("hp")
        sA = A.tile([2, H + T], F32)
        sB = A.tile([2, H + T], F32)
        R_sA, R_sB = Res("sA"), Res("sB")
        pooled = A.tile([8, T], BF16)
        R_pl = Res("pooled")
        pw = A.tile([4, 2, 256], BF16)
        R_pw = Res("pw")
        t15 = A.tile([16], F32)
        R_t15 = Res("t15")
        P.dma("sp", pw, poolws[j].rearrange("p (g kc d) -> p g kc d", g=4, kc=2), [R_poolws[j]], [R_pw])
        P.memset("dve", hp[:, :, 0:H], 0.0, [R_hp])
        L = H + T
        for t in range(S // T):
            t0 = t * T
            if t == 0 and _os.environ.get("DUMMY"):
                rms_rstd(int(_os.environ.get("DUMMY")) * T, T)
            norm_mod(t0, T, lv[:, i, 0, :], lv[:, i, 1, :], hp, R_hp, dst_off=H)
            for g in range(4):
                cs = slice(2 * g, 2 * g + 2)
                src, rsrc = hp[:, cs, :], R_hp
                bufs = [(sA, R_sA), (sB, R_sB)]
                sh = 1
                for step in range(g + 1):
                    dst, rdst = bufs[step % 2]
                    eng = "dve" if (step % 2 == 0 or _os.environ.get("NOPOOL")) else "pool"
                    P.tt(eng, dst[:, :, sh:L], src[:, :, sh:L], src[:, :, 0:L - sh], ALU.add,
                         [rsrc], [rdst])
                    if sh > 1 or True:
                        pass
                    src, rsrc = dst, rdst
                    sh *= 2
                win = 2 ** (g + 1)
                P.stt("dve", pooled[:, cs, :], src[:, :, H:L], 1.0 / win, hp[:, cs, H:L],
                      ALU.mult, ALU.subtract, [rsrc, R_hp], [R_pl])
                if t == 0:
                    for cc in range(2):
                        c = 2 * g + cc
                        P.tt("dve", t15[:, 0:15], src[:, cc, H:H + 15],
                             vecs[:, V_INVC + 16 * g:V_INVC + 16 * g + 15], ALU.mult,
                             [rsrc, R_vecs], [R_t15])
                        P.tt("dve", pooled[:, c, 0:15], t15[:, 0:15], hp[:, c, H:H + 15], ALU.subtract,
                             [R_t15, R_hp], [R_pl])
            if t + 1 < S // T:
                P.copy("dve", hp[:, :, 0:H], hp[:, :, T:T + H], [R_hp], [R_hp])
            for g in range(4):
                for dc in range(2):
                    bk = dn_bank()
                    for kc in range(2):
                        P.mm(psb[bk][:, 0:T], pw[:, g, kc, dc * 128:(dc + 1) * 128], pooled[:, 2 * g + kc, :],
                             kc == 0, kc == 1, [R_pw, R_pl], [R_ps[bk]])
                    resid_add(t0, T, 2 * g + dc, bk, lv[:, i, 2, :])
        end_layer([R_hp, R_sA, R_sB, R_pl, R_pw, R_t15])


    def sgu_layer(i):
        T = 256
        begin_layer(T, 4)
        wsT = A.tile([8, 128], BF16)
        R_ws = Res("wsT")
        bterm = A.tile([8, 128], F32)
        R_bt = Res("bterm")
        hT = [A.tile([8, T], BF16) for _ in range(2)]
        R_h = [Res("sh0"), Res("sh1")]
        ub = [A.tile([T], F32) for _ in range(2)]
        R_u = [Res("u0"), Res("u1")]
        tm = [A.tile([T], F32) for _ in range(2)]
        R_tm = [Res("tm0"), Res("tm1")]
        vbuf = A.tile([1024], F32)
        R_v = Res("vbuf")
        vn = A.tile([2, 1024], BF16)
        R_vn = [Res("vn0"), Res("vn1")]
        gT = A.tile([8, T], BF16)
        R_g = Res("gT")
        st = A.tile([8], F32)
        R_st = Res("st")
        lng = vecs[:, V_LNG:V_LNG + 8]
        lnb = vecs[:, V_LNB:V_LNB + 8]
        vb3 = vbuf.rearrange("p (h t) -> p h t", h=8)
        P.dma("sp", vb3, sgu_wsT_d.rearrange("h s t -> s h t"), (), [R_v])
        for h in range(8):
            P.tt("dve", wsT[:, h, :], vb3[:, h, :], cmat[:, 128:256], ALU.mult, [R_v, R_cmat], [R_ws])
        bs_bc = bass.AP(sgu_bs_d.tensor, 0, [[0, 128], [1, 1024]])
        P.dma("sp", bterm.rearrange("p h t -> p (h t)"), bs_bc, (), [R_bt])
        for h in range(8):
            bk = up_bank()
            P.mm(psb[bk][:, 0:128], ones_1, wsT[:, h, :], True, True, [R_c, R_ws], [R_ps[bk]])
            P.stt("dve", bterm[:, h, :], psb[bk][:, 0:128], lnb[:, h:h + 1], bterm[:, h, :],
                  ALU.mult, ALU.add, [R_ps[bk], R_vecs, R_bt], [R_bt])
        nu = 0
        for t in range(S // T):
            t0 = t * T
            h_, rh = hT[t % 2], R_h[t % 2]
            norm_mod(t0, T, lv[:, i, 0, :], lv[:, i, 1, :], h_, rh)
            for ch in range(2):
                P.memset("dve", st[:, 0:4], 0.0, [R_st])
                for half in range(2):
                    wp, rwp = wload(sgu_wins[2 + half], R_sgu_wins)
                    wp = wp.rearrange("p (kc c) -> p kc c", kc=8)
                    bk = up_bank()
                    for kc in range(8):
                        P.mm(psb[bk][:, 0:512], h_[:, kc, ch * 128:(ch + 1) * 128], wp[:, kc, :],
                             kc == 0, kc == 7, [rh, rwp], [R_ps[bk]])
                    P.act(vbuf[:, half * 512:(half + 1) * 512], psb[bk][:, 0:512], AF.Gelu,
                          [R_ps[bk], R_st], [R_v, R_st], accum_out=st[:, half:half + 1])
                P.tt("dve", st[:, 4:5], st[:, 0:1], st[:, 1:2], ALU.add, [R_st], [R_st])
                P.ts("dve", st[:, 4:5], st[:, 4:5], -1.0 / 1024, None, ALU.mult, None, [R_st], [R_st])
                P.act(vn[:, ch, :], vbuf, AF.Square, [R_v, R_st], [R_vn[ch], R_st],
                      bias=st[:, 4:5], scale=1.0, accum_out=st[:, 2:3])
                P.act(st[:, 5:6], st[:, 2:3], AF.Sqrt, [R_st, R_vecs], [R_st], bias=eps_ln, scale=1.0 / 1024)
                P.recip(st[:, 6:7], st[:, 5:6], [R_st], [R_st])
                P.ts("dve", vn[:, ch, :], vbuf, st[:, 4:5], st[:, 6:7], ALU.add, ALU.mult,
                     [R_v, R_st], [R_vn[ch]])
            for h in range(8):
                if h % 4 == 0:
                    up_, rup = wload(sgu_wins[h // 4], R_sgu_wins)
                    up_ = up_.rearrange("p (kc c) -> p kc c", kc=8)
                bk = up_bank()
                for kc in range(8):
                    P.mm(psb[bk][:, 0:T], up_[:, kc, (h % 4) * 128:(h % 4 + 1) * 128], h_[:, kc, :],
                         kc == 0, kc == 7, [rup, rh], [R_ps[bk]])
                u_, ru = ub[nu % 2], R_u[nu % 2]
                tm_, rtm = tm[nu % 2], R_tm[nu % 2]
                nu += 1
                P.act(u_, psb[bk][:, 0:T], AF.Gelu, [R_ps[bk]], [ru])
                bk2 = dn_bank()
                for ch in range(2):
                    P.mm(psb[bk2][:, ch * 128:(ch + 1) * 128], vn[:, ch, h * 128:(h + 1) * 128], wsT[:, h, :],
                         True, True, [R_vn[ch], R_ws], [R_ps[bk2]])
                for ch in range(2):
                    P.stt("dve", tm_[:, ch * 128:(ch + 1) * 128], psb[bk2][:, ch * 128:(ch + 1) * 128],
                          lng[:, h:h + 1], bterm[:, h, :], ALU.mult, ALU.add,
                          [R_ps[bk2], R_vecs, R_bt], [rtm])
                P.tt("dve", gT[:, h, :], tm_, u_, ALU.mult, [rtm, ru], [R_g])
            for half in range(2):
                wo_, rwo = wload(sgu_wouts[half], R_sgu_wouts)
                wo_ = wo_.rearrange("p (kc c) -> p kc c", kc=8)
                for dd in range(4):
                    bk = dn_bank()
                    for kc in range(8):
                        P.mm(psb[bk][:, 0:T], wo_[:, kc, dd * 128:(dd + 1) * 128], gT[:, kc, :],
                             kc == 0, kc == 7, [rwo, R_g], [R_ps[bk]])
                    resid_add(t0, T, half * 4 + dd, bk, lv[:, i, 2, :])
        end_layer([R_ws, R_bt, R_v, R_g, R_st] + R_h + R_u + R_tm + R_vn)


    def bc_mid(a, n):
        ap = [list(x) for x in a.ap]
        return bass.AP(a.tensor, a.offset, [ap[0], [0, n]] + ap[1:])

    def mla_layer(i):
        T = 128
        NT = S // T
        begin_layer(T, 3)
        tmp, R_tmp, rstd, R_rstd = L["tmp"], L["R_tmp"], L["rstd"], L["R_rstd"]
        SM = float(192 ** -0.5)
        ckvT = A.tile([S], BF16)
        krT = A.tile([S], BF16)
        Vtok = A.tile([NT, 128], BF16)
        R_kv = [Res(f"kv{t}") for t in range(NT)]
        hT = A.tile([8, T], BF16)
        R_h = Res("mh")
        sqq = A.tile([384], BF16)
        R_sqq = Res("sqq")
        rq = A.tile([512], F32)
        R_rq = Res("rq")
        cqn = A.tile([2, T], BF16)
        R_cqn = Res("cqn")
        posf = A.tile([T], F32)
        thc = A.tile([T], F32)
        ths = A.tile([T], F32)
        t1 = A.tile([T], F32)
        R_rope = Res("rope")
        R_cs = Res("cossin")
        fa = A.tile([512], F32)
        fb = A.tile([512], F32)
        R_fa, R_fb = Res("fa"), Res("fb")
        qn_g = A.tile([4, T], BF16)
        qp_g = A.tile([4, T], BF16)
        qr_g = A.tile([4, T], BF16)
        on_g = A.tile([4, T], BF16)
        at_g = A.tile([4, T], BF16)
        R_qn, R_qp, R_qr, R_on, R_at = Res("qn"), Res("qp"), Res("qr"), Res("on"), Res("at")
        PT = [A.tile([512], BF16) for _ in range(2)]
        R_PT = [Res("PT0"), Res("PT1")]
        negm = A.tile([4, 128], BF16)
        R_negm = Res("negm")
        for hh in range(4):
            P.copy("dve", negm[:, hh, :], cmat[:, 256:384], [R_cmat], [R_negm])
        qng = vecs[:, V_QNG:V_QNG + 2]
        kvng = vecs[:, V_KVNG:V_KVNG + 1]
        fcos = vecs[:, V_FREQ:V_FREQ + 1]
        fsin = vecs[:, V_FREQ + 1:V_FREQ + 2]
        MAGIC = 12582912.0
        C1 = 6.28125
        C2 = float(2 * np.pi - 6.28125)
        PI = float(np.pi)
        S_BANKS = [4, 5]
        ns = 0

        def sin_of(dst, freq_col, shift):
            d = dst[0:64]
            n_ = t1[0:64]
            P.ts("dve", d, posf[0:64], freq_col[0:64], shift, ALU.mult, ALU.add, [R_rope, R_vecs], [R_cs])
            P.ts("dve", n_, d, 1.0 / (2 * PI), MAGIC, ALU.mult, ALU.add, [R_cs], [R_rope])
            P.ts("dve", n_, n_, -MAGIC, None, ALU.add, None, [R_rope], [R_rope])
            P.stt("dve", d, n_, -C1, d, ALU.mult, ALU.add, [R_rope, R_cs], [R_cs])
            P.stt("dve", d, n_, -C2, d, ALU.mult, ALU.add, [R_rope, R_cs], [R_cs])
            P.ts("dve", d, d, -PI, PI, ALU.max, ALU.min, [R_cs], [R_cs])
            P.act(d, d, AF.Sin, [R_cs], [R_cs])

        def rope(dst_bf, src_ps, R_src, nh, R_dst):
            n = nh * T
            a_, b_ = fa[0:64, 0:n], fb[:, 0:n]
            if nh == 1:
                cosb, sinb_lo, sinb_hi = thc[0:64], ths[0:32], ths[32:64]
                s3 = lambda ap: ap
            else:
                cosb = bc_mid(thc[0:64], nh)
                sinb_lo, sinb_hi = bc_mid(ths[0:32], nh), bc_mid(ths[32:64], nh)
                s3 = lambda ap: ap.rearrange("p (h t) -> p h t", h=nh)
            P.tt("dve", s3(a_), s3(src_ps[0:64, 0:n]), cosb, ALU.mult, [R_src, R_cs], [R_fa])
            P.tt("dve", s3(b_[0:32]), s3(src_ps[32:64, 0:n]), sinb_hi, ALU.mult, [R_src, R_cs], [R_fb])
            P.tt("dve", s3(b_[32:64]), s3(src_ps[0:32, 0:n]), sinb_lo, ALU.mult, [R_src, R_cs], [R_fb])
            P.tt("dve", dst_bf, a_, b_[0:64], ALU.add, [R_fa, R_fb], [R_dst])

        for it in range(NT):
            t0 = it * T
            cols = slice(t0, t0 + T)
            norm_mod(t0, T, lv[:, i, 0, :], lv[:, i, 1, :], hT, R_h)
            wdq, rwdq = wload(mla_wdqs, R_mla_wdqs)
            wdq = wdq[:, 0:8 * 448].rearrange("p (kc c) -> p kc c", kc=8)
            bx = up_bank()
            for fc in range(4):
                M = 128 if fc < 3 else 64
                for kc in range(8):
                    P.mm(psb[bx][0:M, fc * 128:(fc + 1) * 128], wdq[:, kc, fc * 128:fc * 128 + M], hT[:, kc, :],
                         kc == 0, kc == 7, [rwdq, R_h], [R_ps[bx]])
            pos_bc = bass.AP(pos_d.tensor, t0, [[0, 64], [1, T]])
            P.dma("sp", posf[0:64].bitcast(I32), pos_bc, (), [R_rope])
            P.copy("dve", posf[0:64], posf[0:64].bitcast(I32), [R_rope], [R_rope])
            sin_of(thc, fcos, PI / 2)
            sin_of(ths, fsin, 0.0)
            P.act(sqq[:, 0:384], psb[bx][:, 0:384], AF.Square, [R_ps[bx]], [R_sqq])
            P.mm(psb[0][:, 0:128], ones_q, sqq[:, 0:128], True, False, [R_c, R_sqq], [R_ps[0]])
            P.mm(psb[0][:, 0:128], ones_q, sqq[:, 128:256], False, True, [R_c, R_sqq], [R_ps[0]])
            P.mm(psb[0][:, 128:256], ones_kv, sqq[:, 256:384], True, True, [R_c, R_sqq], [R_ps[0]])
            P.act(rq[:, 0:256], psb[0][:, 0:256], AF.Sqrt, [R_ps[0], R_vecs], [R_rq], bias=eps_rms, scale=1.0)
            P.recip(rq[:, 256:512], rq[:, 0:256], [R_rq], [R_rq])
            for kc in range(2):
                P.stt("dve", cqn[:, kc, :], psb[bx][:, kc * 128:(kc + 1) * 128], qng[:, kc:kc + 1],
                      rq[:, 256:384], ALU.mult, ALU.mult, [R_ps[bx], R_vecs, R_rq], [R_cqn])
            P.stt("dve", ckvT[:, cols], psb[bx][:, 256:384], kvng, rq[:, 384:512], ALU.mult, ALU.mult,
                  [R_ps[bx], R_vecs, R_rq], [R_kv[it]])
            rope(krT[0:64, cols], psb[bx][:, 384:512], R_ps[bx], 1, R_kv[it])
            bv = up_bank()
            P.mm(psb[bv][:, 0:128], ckvT[:, cols], ident, True, True, [R_kv[it], R_c], [R_ps[bv]])
            P.act(Vtok[:, it, :], psb[bv][:, 0:128], AF.Copy, [R_ps[bv]], [R_kv[it]])
            for g in range(4):
                gp, rgp = wload(mla_grps[g], R_mla_grps)
                wuq_g = gp[:, 0:1536].rearrange("p (kc c) -> p kc c", kc=2)
                wukT_g = gp[:, 1536:2048].rearrange("p (h c) -> p h c", h=4)
                wuv_g = gp[:, 2048:2560].rearrange("p (h c) -> p h c", h=4)
                wo_, rwo = wload(mla_wos[g], R_mla_wos)
                wo_ = wo_.rearrange("p (h d) -> p h d", h=4)
                bq = up_bank()
                for hh in range(4):
                    for kc in range(2):
                        P.mm(psb[bq][:, hh * 128:(hh + 1) * 128], wuq_g[:, kc, hh * 192:hh * 192 + 128], cqn[:, kc, :],
                             kc == 0, kc == 1, [rgp, R_cqn], [R_ps[bq]])
                P.act(qn_g.rearrange("p h t -> p (h t)"), psb[bq][:, 0:512], AF.Copy, [R_ps[bq]], [R_qn])
                br = up_bank()
                for hh in range(4):
                    for kc in range(2):
                        P.mm(psb[br][0:64, hh * 128:(hh + 1) * 128], wuq_g[:, kc, hh * 192 + 128:hh * 192 + 192],
                             cqn[:, kc, :], kc == 0, kc == 1, [rgp, R_cqn], [R_ps[br]])
                rope(qr_g[0:64].rearrange("p h t -> p (h t)"), psb[br], R_ps[br], 4, R_qr)
                ba = up_bank()
                for hh in range(4):
                    P.mm(psb[ba][:, hh * 128:(hh + 1) * 128], wukT_g[:, hh, :], qn_g[:, hh, :], True, True,
                         [rgp, R_qn], [R_ps[ba]])
                P.act(qp_g.rearrange("p h t -> p (h t)"), psb[ba][:, 0:512], AF.Copy, [R_ps[ba]], [R_qp])
                qp2 = qp_g.rearrange("p h t -> p (h t)")
                qr2 = qr_g[0:64].rearrange("p h t -> p (h t)")
                BO, BL = 6, 7

                def qk(j):
                    nonlocal ns
                    bs_ = S_BANKS[ns % 2]
                    pt_, rpt = PT[ns % 2], R_PT[ns % 2]
                    ns += 1
                    kc_ = slice(j * 128, (j + 1) * 128)
                    P.mm(psb[bs_][:, 0:512], ckvT[:, kc_], qp2, True, False, [R_kv[j], R_qp], [R_ps[bs_]])
                    P.mm(psb[bs_][:, 0:512], krT[0:64, kc_], qr2, False, j < it, [R_kv[j], R_qr], [R_ps[bs_]])
                    if j == it:
                        P.mm(psb[bs_][:, 0:512], ident, negm.rearrange("p h t -> p (h t)"), False, True,
                             [R_c, R_negm], [R_ps[bs_]])
                    P.act(pt_, psb[bs_][:, 0:512], AF.Exp, [R_ps[bs_]], [rpt], scale=SM)
                    return pt_, rpt

                def pv(j, pt_, rpt):
                    P.mm(psb[BO][:, 0:512], Vtok[:, j, :], pt_, j == 0, j == it, [R_kv[j], rpt], [R_ps[BO]])
                    P.mm(psb[BL][:, 0:512], ones_1, pt_, j == 0, j == it, [R_c, rpt], [R_ps[BL]])

                prev = None
                for j in range(it + 1):
                    cur = qk(j)
                    if prev is not None:
                        pv(j - 1, *prev)
                    prev = cur
                pv(it, *prev)
                P.recip(fa, psb[BL][:, 0:512], [R_ps[BL]], [R_fa])
                P.tt("dve", on_g.rearrange("p h t -> p (h t)"), psb[BO][:, 0:512], fa, ALU.mult,
                     [R_ps[BO], R_fa], [R_on])
                bu = up_bank()
                for hh in range(4):
                    P.mm(psb[bu][:, hh * 128:(hh + 1) * 128], wuv_g[:, hh, :], on_g[:, hh, :], True, True,
                         [rgp, R_on], [R_ps[bu]])
                P.act(at_g.rearrange("p h t -> p (h t)"), psb[bu][:, 0:512], AF.Copy, [R_ps[bu]], [R_at])
                for half in range(2):
                    bo_ = up_bank()
                    for dd in range(4):
                        d = half * 4 + dd
                        for hh in range(4):
                            P.mm(psb[bo_][:, dd * 128:(dd + 1) * 128], wo_[:, hh, d * 128:(d + 1) * 128],
                                 at_g[:, hh, :], hh == 0, hh == 3, [rwo, R_at], [R_ps[bo_]])
                    for dd in range(4):
                        d = half * 4 + dd
                        P.stt("dve", xT[:, d, cols], psb[bo_][:, dd * 128:(dd + 1) * 128],
                              lv[:, i, 2, d:d + 1], xT[:, d, cols], ALU.mult, ALU.add,
                              [R_ps[bo_], R_lv] + xres(t0, T), xres(t0, T))
        end_layer(R_kv + [R_h, R_sqq, R_rq, R_cqn, R_rope, R_cs, R_fa, R_fb, R_qn, R_qp, R_qr, R_on, R_at,
                          R_negm] + R_PT)

    def final_layer():
        T = 512
        begin_layer(T, 0)
        tmp, R_tmp, rstd, R_rstd = L["tmp"], L["R_tmp"], L["rstd"], L["R_rstd"]
        stg = [A.tile([T], F32) for _ in range(4)]
        R_stg = [Res(f"stg{i}") for i in range(4)]
        ov = out_d.rearrange("(c p) t -> p c t", p=128)
        R_out = Res("out")
        n = 0
        for t in range(S // T):
            t0 = t * T
            rms_rstd(t0, T)
            for c in range(NCH):
                b = cnt["tmp"] % 2
                cnt["tmp"] += 1
                P.tt("dve", tmp[b][:, 0:T], xT[:, c, t0:t0 + T], rstd[:, 0:T], ALU.mult,
                     xres(t0, T) + [R_rstd], [R_tmp[b]])
                s_, rs_ = stg[n % 4], R_stg[n % 4]
                n += 1
                P.act(s_, tmp[b][:, 0:T], AF.Identity, [R_tmp[b], R_vecs], [rs_],
                      scale=vecs[:, V_FINALG + c:V_FINALG + c + 1])
                P.dma("sp", ov[:, c, t0:t0 + T], s_, [rs_], [R_out], group=True)
        P.op("sp", lambda e: e.nop(), [R_out], [])
        end_layer(R_stg)

    def raw_store():
        ov = out_d.rearrange("(c p) t -> p c t", p=128)
        R_out = Res("out")
        for t in range(8):
            P.dma("sp", ov[:, :, t * 512:(t + 1) * 512], xT[:, :, t * 512:(t + 1) * 512],
                  xres(t * 512, 512), [R_out], group=True)
        P.op("sp", lambda e: e.nop(), [R_out], [])

    for i in range(n_layers):
        derive(i)
        kind = i % 3
        if kind == 0:
            pool_layer(i)
        elif kind == 1:
            sgu_layer(i)
        else:
            mla_layer(i)
        if not (_os.environ.get("SKIP_MLP") and i == n_layers - 1):
            mlp_layer(i)
    if final:
        final_layer()
    else:
        raw_store()

    nwait = P.emit()
    info = {"ops": len(P.ops), "waits": nwait, "sems": P.nsem, "arena_peak_words": A.peak, "log": P.log}
    return nc, stack, info


def _pcol(v, ncol):
    return np.ascontiguousarray(np.asarray(v, np.float32).reshape(ncol, 128).T)


def prep_shared(inp):
    f = lambda a: np.ascontiguousarray(np.asarray(a, np.float32))
    sh = {}
    sh["ada_w"] = f(inp["ada_w"])
    sh["mlp_w1"] = f(inp["mlp_w1"])
    sh["mlp_w2"] = f(inp["mlp_w2"])
    sh["pool_w"] = f(inp["pool_w"])
    sh["sgu_w_in"] = f(inp["sgu_w_in"][0])
    sh["sgu_w_out"] = f(inp["sgu_w_out"][0])
    sh["sgu_wsT"] = f(np.transpose(np.asarray(inp["sgu_w_s"][0]), (0, 2, 1)))
    sh["sgu_ln"] = f(np.stack([np.asarray(inp["sgu_ln_g"][0]), np.asarray(inp["sgu_ln_b"][0])]))
    sh["sgu_bs"] = f(np.asarray(inp["sgu_b_s"][0]).reshape(1, 1024))
    sh["mla_wdq"] = f(inp["mla_w_dq_dkv"][0])
    sh["mla_wuq"] = f(inp["mla_w_uq"][0])
    wukv = np.asarray(inp["mla_w_ukv"][0], np.float32).reshape(128, 16, 256)
    sh["mla_wukT"] = f(np.transpose(wukv[:, :, :128], (1, 2, 0)))
    sh["mla_wuv"] = f(wukv[:, :, 128:])
    sh["mla_wo"] = f(inp["mla_w_o"][0])
    idx = np.arange(128)
    cm = np.zeros((128, 384), np.float32)
    cm[:, 0:128] = np.eye(128, dtype=np.float32)
    cm[:, 128:256] = (idx[:, None] <= idx[None, :]).astype(np.float32)
    cm[:, 256:384] = np.where(idx[:, None] <= idx[None, :], 0.0, NEG).astype(np.float32)
    sh["cmat"] = cm
    return sh


def prep_vecs(inp, b):
    v = np.zeros((128, NV), np.float32)
    for i in range(DEPTH):
        o = i * V_LAYER
        v[:, o:o + 48] = _pcol(inp["ada_b"][i], 48)
        v[:, o + 48:o + 56] = _pcol(inp["norm_mix_g"][i], 8)
        v[:, o + 56:o + 64] = _pcol(inp["norm_mlp_g"][i], 8)
    for j in range(2):
        v[:, V_POOLSC + 8 * j:V_POOLSC + 8 * j + 8] = _pcol(inp["pool_scale"][j], 8)
    v[:, V_FINALG:V_FINALG + 8] = _pcol(inp["final_g"], 8)
    v[:, V_QNG:V_QNG + 2] = _pcol(inp["mla_q_norm_g"][0], 2)
    v[:, V_KVNG:V_KVNG + 1] = _pcol(inp["mla_kv_norm_g"][0], 1)
    v[:, V_C:V_C + 8] = _pcol(inp["c"][b], 8)
    fr = (10000.0 ** (-np.arange(0, 64, 2, dtype=np.float32) / 64)).astype(np.float32)
    p = np.arange(128)
    v[:, V_FREQ] = fr[p % 32]
    v[:, V_FREQ + 1] = np.where((p % 64) < 32, fr[p % 32], -fr[p % 32])
    for g, win in enumerate((2, 4, 8, 16)):
        v[:, V_INVC + 16 * g:V_INVC + 16 * g + 16] = (1.0 / np.minimum(np.arange(16) + 1.0, float(win))).astype(np.float32)[None, :]
    v[:, V_EPS] = RMS_EPS
    v[:, V_EPS + 1] = LN_EPS
    v[:, V_HALFPI] = np.float32(np.pi / 2)
    v[:, V_LNG:V_LNG + 8] = _pcol(inp["sgu_ln_g"][0], 8)
    v[:, V_LNB:V_LNB + 8] = _pcol(inp["sgu_ln_b"][0], 8)
    return v


def make_in_maps(inp, cores):
    sh = prep_shared(inp)
    maps = []
    for b in cores:
        m = dict(sh)
        m["xT"] = np.ascontiguousarray(np.asarray(inp["x"][b], np.float32).T)
        m["vecs"] = prep_vecs(inp, b)
        m["pos"] = np.ascontiguousarray(np.asarray(inp["positions"][b], np.int32).reshape(1, S))
        maps.append(m)
    return maps


_CACHE = {}


def kernel(**inputs):
    if "nc" not in _CACHE:
        _CACHE["nc"] = build()
    nc, stack, info = _CACHE["nc"]
    in_maps = make_in_maps(inputs, list(range(8)))
    res = run_bass_kernel_spmd(nc, in_maps, core_ids=list(range(8)))
    out = np.stack([np.ascontiguousarray(res.results[b]["outT"].T) for b in range(8)], axis=0)
    return out.astype(np.float32)
```

```python
import numpy as np
import os as _os
from contextlib import ExitStack
import concourse.bass as bass
import concourse.mybir as mybir
from concourse.bass_utils import run_bass_kernel_spmd

F32 = mybir.dt.float32
BF16 = mybir.dt.bfloat16
I32 = mybir.dt.int32
AF = mybir.ActivationFunctionType
ALU = mybir.AluOpType
AX = mybir.AxisListType

D = 1024
S = 4096
NCH = 8
DFF = 4096
DEPTH = 4
RMS_EPS = 1e-6
LN_EPS = 1e-5
NEG = -30000.0

V_LAYER = 64
V_POOLSC = 256
V_FINALG = 272
V_QNG = 280
V_KVNG = 282
V_C = 283
V_FREQ = 291
V_INVC = 293
V_EPS = 357
V_HALFPI = 359
V_LNG = 360
V_LNB = 368
NV = 376


class Res:
    __slots__ = ("name", "w", "wg", "r", "rd", "sem", "cnt")

    def __init__(self, name):
        self.name = name
        self.w = None
        self.wg = []
        self.r = {}
        self.rd = []
        self.sem = None
        self.cnt = 0


class Op:
    __slots__ = ("eng", "fn", "deps", "signal", "sem", "token", "dma", "wres", "idx")

    def __init__(self, eng, fn, dma):
        self.idx = 0
        self.eng = eng
        self.fn = fn
        self.dma = dma
        self.deps = []
        self.signal = False
        self.sem = None
        self.token = 0
        self.wres = None


class Prog:
    def __init__(self, nc, stack):
        self.nc = nc
        self.stack = stack
        self.ops = []
        self.engs = {"pe": nc.tensor, "act": nc.scalar, "dve": nc.vector,
                     "pool": nc.gpsimd, "sp": nc.sync}
        self.nsem = 0
        self.eidx = {k: 0 for k in self.engs}
        self.log = []
        self.fence_ops = []

    def newsem(self, name):
        self.nsem += 1
        return self.stack.enter_context(self.nc.semaphore(f"{name}_{self.nsem}"))

    def op(self, eng, fn, R=(), W=(), dma=False, group=False):
        o = Op(eng, fn, dma)
        self.eidx[eng] += 1
        o.idx = self.eidx[eng]
        deps = set()
        raw = set()
        for r in R:
            if r.w is not None:
                deps.add(r.w)
                raw.add(r.w)
            for g in r.wg:
                deps.add(g)
        for w in W:
            if w.w is not None:
                deps.add(w.w)
            if not group:
                for g in w.wg:
                    deps.add(g)
            for ro in w.r.values():
                deps.add(ro)
            for ro in w.rd:
                deps.add(ro)
        for f in self.fence_ops:
            deps.add(f)
        o.deps = [d for d in deps if d.dma or d.eng != eng or eng != "pe"]
        for d in o.deps:
            d.signal = True
        for r in R:
            if dma:
                r.rd.append(o)
            else:
                r.r[eng] = o
        for w in W:
            if group:
                w.wg.append(o)
            else:
                w.w = o
                w.wg = []
            w.r = {}
            w.rd = []
        if dma:
            o.wres = W[0]
        self.ops.append(o)
        return o

    def fence(self, res_list):
        ops = set()
        for r in res_list:
            if r.w is not None:
                ops.add(r.w)
            ops.update(r.wg)
            ops.update(r.r.values())
            ops.update(r.rd)
        self.fence_ops = list(ops)

    def emit(self):
        esem = {k: self.newsem("e_" + k) for k in self.engs}
        ecnt = {k: 0 for k in self.engs}
        seen = {k: {} for k in self.engs}
        nwait = 0
        for o in self.ops:
            e = self.engs[o.eng]
            sn = seen[o.eng]
            for d in o.deps:
                key = id(d.sem)
                if sn.get(key, 0) < d.token:
                    e.wait_ge(d.sem, d.token)
                    sn[key] = d.token
                    nwait += 1
            ins = o.fn(e)
            if _os.environ.get("DUMPOPS"):
                self.log.append((o.eng, o.idx, o.dma, [(d.eng, d.idx, d.token) for d in o.deps], o.signal, str(ins)[:150]))
            if o.dma:
                r = o.wres
                if r.sem is None:
                    r.sem = self.newsem("d_" + r.name)
                r.cnt += 16
                ins.then_inc(r.sem, 16)
                o.sem = r.sem
                o.token = r.cnt
            elif o.signal:
                ecnt[o.eng] += 1
                ins.then_inc(esem[o.eng], 1)
                o.sem = esem[o.eng]
                o.token = ecnt[o.eng]
        return nwait

    def mm(self, out, lhsT, rhs, start, stop, R, W):
        return self.op("pe", lambda e: e.matmul(out, lhsT, rhs, start=start, stop=stop), R, W)

    def act(self, out, in_, func, R, W, bias=None, scale=None, accum_out=None):
        kw = {}
        if bias is not None:
            kw["bias"] = bias
        if scale is not None:
            kw["scale"] = scale
        if accum_out is not None:
            kw["accum_out"] = accum_out
        return self.op("act", lambda e: e.activation(out, in_, func, **kw), R, W)

    def tt(self, eng, out, in0, in1, op, R, W):
        return self.op(eng, lambda e: e.tensor_tensor(out, in0, in1, op), R, W)

    def ts(self, eng, out, in0, s1, s2, op0, op1, R, W):
        if s2 is None:
            return self.op(eng, lambda e: e.tensor_single_scalar(out, in0, s1, op0), R, W)
        return self.op(eng, lambda e: e.tensor_scalar(out, in0, s1, s2, op0, op1), R, W)

    def stt(self, eng, out, in0, scalar, in1, op0, op1, R, W):
        return self.op(eng, lambda e: e.scalar_tensor_tensor(out, in0, scalar, in1, op0, op1), R, W)

    def copy(self, eng, out, in_, R, W):
        return self.op(eng, lambda e: e.tensor_copy(out, in_), R, W)

    def memset(self, eng, ap, val, W):
        return self.op(eng, lambda e: e.memset(ap, val), (), W)

    def recip(self, out, in_, R, W):
        return self.op("dve", lambda e: e.reciprocal(out, in_), R, W)

    def dma(self, eng, out, in_, R, W, group=False):
        return self.op(eng, lambda e: e.dma_start(out=out, in_=in_), R, W, dma=True, group=group)


class Arena:
    def __init__(self, ap, nwords):
        self.ap = ap
        self.n = nwords
        self.off = 0
        self.marks = []
        self.peak = 0

    def push(self):
        self.marks.append(self.off)

    def pop(self):
        self.off = self.marks.pop()

    def tile(self, free, dt, parts=128):
        sz = 4 if dt in (F32, I32) else 2
        n = 1
        for f in free:
            n *= f
        words = (n * sz + 3) // 4
        words = (words + 7) // 8 * 8
        assert self.off + words <= self.n, f"arena overflow {self.off}+{words}>{self.n}"
        a = self.ap[:, self.off:self.off + words]
        self.off += words
        self.peak = max(self.peak, self.off)
        if dt != F32:
            a = a.bitcast(dt)
        a = a[:, 0:n]
        if len(free) == 2:
            a = a.rearrange("p (a b) -> p a b", a=free[0])
        elif len(free) == 3:
            a = a.rearrange("p (a b c) -> p a b c", a=free[0], b=free[1])
        if parts != 128:
            a = a[0:parts]
        return a


ARENA_WORDS = 20416


def build(n_layers=DEPTH, final=True):
    nc = bass.Bass("TRN2", target_bir_lowering=False)
    stack = ExitStack()
    P = Prog(nc, stack)

    def dram_in(name, shape, dt=F32):
        return nc.dram_tensor(name, list(shape), dt, kind="ExternalInput").ap()

    def dram_tmp(name, shape, dt=BF16):
        return nc.dram_tensor(name, list(shape), dt, kind="Internal").ap()

    xT_d = dram_in("xT", [D, S])
    vecs_d = dram_in("vecs", [128, NV])
    cmat_d = dram_in("cmat", [128, 384])
    pos_d = dram_in("pos", [1, S], I32)
    ada_w_d = dram_in("ada_w", [DEPTH, D, 6 * D])
    w1_d = dram_in("mlp_w1", [DEPTH, D, DFF])
    w2_d = dram_in("mlp_w2", [DEPTH, DFF, D])
    poolw_d = dram_in("pool_w", [2, 4, 256, 256])
    sgu_win_d = dram_in("sgu_w_in", [D, 2 * D])
    sgu_wout_d = dram_in("sgu_w_out", [D, D])
    sgu_wsT_d = dram_in("sgu_wsT", [8, 128, 128])
    sgu_ln_d = dram_in("sgu_ln", [2, D])
    sgu_bs_d = dram_in("sgu_bs", [1, D])
    mla_wdq_d = dram_in("mla_wdq", [D, 448])
    mla_wuq_d = dram_in("mla_wuq", [256, 3072])
    mla_wukT_d = dram_in("mla_wukT", [16, 128, 128])
    mla_wuv_d = dram_in("mla_wuv", [128, 16, 128])
    mla_wo_d = dram_in("mla_wo", [2048, D])
    out_d = nc.dram_tensor("outT", [D, S], F32, kind="ExternalOutput").ap()

    w1s = [dram_tmp(f"w1s{i}", [8, 128, 4096]) for i in range(DEPTH)]
    w2s = [dram_tmp(f"w2s{i}", [8, 128, 4096]) for i in range(DEPTH)]
    poolws = [dram_tmp(f"poolws{j}", [128, 2048]) for j in range(2)]
    R_w1s = [Res(f"w1s{i}") for i in range(DEPTH)]
    R_w2s = [Res(f"w2s{i}") for i in range(DEPTH)]
    R_poolws = [Res(f"poolws{j}") for j in range(2)]
    sgu_wins = dram_tmp("sgu_wins", [4, 128, 4096])
    sgu_wouts = dram_tmp("sgu_wouts", [2, 128, 4096])
    R_sgu_wins = Res("sgu_wins")
    R_sgu_wouts = Res("sgu_wouts")
    mla_wdqs = dram_tmp("mla_wdqs", [128, 8 * 448])
    mla_grps = dram_tmp("mla_grps", [4, 128, 2560])
    mla_wos = dram_tmp("mla_wos", [4, 128, 4096])
    R_mla_wdqs = Res("mla_wdqs")
    R_mla_grps = Res("mla_grps")
    R_mla_wos = Res("mla_wos")

    xT = stack.enter_context(nc.sbuf_tensor("xT_sb", [128, NCH, S], F32))
    arena_t = stack.enter_context(nc.sbuf_tensor("arena", [128, ARENA_WORDS], F32))
    A = Arena(arena_t[:, :], ARENA_WORDS)
    psb = [stack.enter_context(nc.psum_tensor(f"ps{i}", [128, 512], F32)) for i in range(8)]
    R_ps = [Res(f"ps{i}") for i in range(8)]
    R_x = [[Res(f"x{c}_{t}") for t in range(32)] for c in range(NCH)]

    def xres_c(c, t0, T):
        return R_x[c][t0 // 128:(t0 + T) // 128]

    def xres(t0, T):
        out = []
        for c in range(NCH):
            out += xres_c(c, t0, T)
        return out

    vecs = A.tile([NV], F32)
    R_vecs = Res("vecs")
    P.dma("sp", vecs, vecs_d, (), [R_vecs])
    ones_m = A.tile([128], BF16)
    R_c = Res("consts")
    P.memset("dve", ones_m, 1.0 / D, [R_c])
    ones_q = A.tile([128], BF16)
    P.memset("dve", ones_q, 1.0 / 256, [R_c])
    ones_kv = A.tile([128], BF16)
    P.memset("dve", ones_kv, 1.0 / 128, [R_c])
    ones_1 = A.tile([128], BF16)
    P.memset("dve", ones_1, 1.0, [R_c])
    cmat = A.tile([384], F32)
    R_cmat = Res("cmat")
    P.dma("sp", cmat, cmat_d, (), [R_cmat])
    ident = A.tile([128], BF16)
    P.copy("dve", ident, cmat[:, 0:128], [R_cmat], [R_c])
    modv = A.tile([DEPTH, 48], F32)
    R_mod = Res("mod")
    eps_rms = vecs[:, V_EPS:V_EPS + 1]
    eps_ln = vecs[:, V_EPS + 1:V_EPS + 2]

    L = {}

    def begin_layer(T, nslot):
        A.push()
        L["ring"] = [A.tile([4096], BF16) for _ in range(nslot)]
        L["R_ring"] = [Res(f"ring{i}") for i in range(nslot)]
        L["ring_i"] = 0
        L["up_banks"] = None
        L["sq"] = [A.tile([T], BF16) for _ in range(2)]
        L["R_sq"] = [Res("sq0"), Res("sq1")]
        L["rt"] = A.tile([T], F32)
        L["R_rt"] = Res("rt")
        L["rstd"] = A.tile([T], F32)
        L["R_rstd"] = Res("rstd")
        L["tmp"] = [A.tile([T], F32) for _ in range(2)]
        L["R_tmp"] = [Res("tmp0"), Res("tmp1")]

    def end_layer(extra):
        P.fence(list(extra) + L["R_ring"] + L["R_sq"] + [L["R_rt"], L["R_rstd"]] + L["R_tmp"])
        A.pop()

    def wload(src_ap, src_res):
        n_ = len(L["ring"])
        i = L["ring_i"] % n_
        L["ring_i"] += 1
        n = 1
        for s_ in src_ap.shape[1:]:
            n *= s_
        P.dma("sp", L["ring"][i][:, 0:n], src_ap, [src_res], [L["R_ring"][i]])
        return L["ring"][i], L["R_ring"][i]

    xv = xT_d.rearrange("(c p) t -> p c t", p=128)
    for t in range(8):
        P.dma("sp", xT[:, :, t * 512:(t + 1) * 512], xv[:, :, t * 512:(t + 1) * 512], (),
              xres(t * 512, 512))

    def cast_mlp(i):
        for q in range(8):
            src = w1_d[i][:, q * 512:(q + 1) * 512].rearrange("(kc p) c -> p kc c", p=128)
            dst = w1s[i][q].rearrange("p (kc c) -> p kc c", kc=8)
            P.dma("pool", dst, src, (), [R_w1s[i]], group=True)
        for q in range(8):
            src = w2_d[i][q * 512:(q + 1) * 512, :].rearrange("(fc p) d -> p fc d", p=128)
            dst = w2s[i][q].rearrange("p (fc d) -> p fc d", fc=4)
            P.dma("pool", dst, src, (), [R_w2s[i]], group=True)

    def cast_pool(j):
        src = poolw_d[j].rearrange("g (kc p) d -> p g kc d", p=128)
        dst = poolws[j].rearrange("p (g kc d) -> p g kc d", g=4, kc=2)
        P.dma("pool", dst, src, (), [R_poolws[j]], group=True)

    def cast_sgu():
        for q in range(4):
            src = sgu_win_d[:, q * 512:(q + 1) * 512].rearrange("(kc p) c -> p kc c", p=128)
            dst = sgu_wins[q].rearrange("p (kc c) -> p kc c", kc=8)
            P.dma("pool", dst, src, (), [R_sgu_wins], group=True)
        for q in range(2):
            src = sgu_wout_d[:, q * 512:(q + 1) * 512].rearrange("(kc p) c -> p kc c", p=128)
            dst = sgu_wouts[q].rearrange("p (kc c) -> p kc c", kc=8)
            P.dma("pool", dst, src, (), [R_sgu_wouts], group=True)

    def cast_mla():
        src = mla_wdq_d.rearrange("(kc p) c -> p kc c", p=128)
        dst = mla_wdqs.rearrange("p (kc c) -> p kc c", kc=8)
        P.dma("pool", dst, src, (), [R_mla_wdqs], group=True)
        for g in range(4):
            src = mla_wuq_d[:, g * 768:(g + 1) * 768].rearrange("(kc p) c -> p kc c", p=128)
            dst = mla_grps[g][:, 0:1536].rearrange("p (kc c) -> p kc c", kc=2)
            P.dma("pool", dst, src, (), [R_mla_grps], group=True)
            src = mla_wukT_d[4 * g:4 * g + 4].rearrange("h n c -> n h c")
            dst = mla_grps[g][:, 1536:2048].rearrange("p (h c) -> p h c", h=4)
            P.dma("pool", dst, src, (), [R_mla_grps], group=True)
            src = mla_wuv_d[:, 4 * g:4 * g + 4, :]
            dst = mla_grps[g][:, 2048:2560].rearrange("p (h c) -> p h c", h=4)
            P.dma("pool", dst, src, (), [R_mla_grps], group=True)
            src = mla_wo_d[g * 512:(g + 1) * 512, :].rearrange("(h p) d -> p h d", p=128)
            dst = mla_wos[g].rearrange("p (h d) -> p h d", h=4)
            P.dma("pool", dst, src, (), [R_mla_wos], group=True)

    cact32 = A.tile([8], F32)
    cact = A.tile([8], BF16)
    R_cact = Res("cact")
    A.push()
    P.act(cact32, vecs[:, V_C:V_C + 8], AF.Silu, [R_vecs], [R_cact])
    P.copy("dve", cact, cact32, [R_cact], [R_cact])
    ada_slots = [A.tile([8, 512], BF16) for _ in range(2)]
    R_ada = [Res(f"ada{i}") for i in range(2)]
    cast_pool(0)
    k = 0
    for i in range(1):
        av = ada_w_d[i].rearrange("(kc p) f -> p kc f", p=128)
        for pc in range(12):
            sl, rs = ada_slots[k % 2], R_ada[k % 2]
            k += 1
            P.dma("pool", sl, av[:, :, pc * 512:(pc + 1) * 512], (), [rs])
            for jj in range(4):
                j = pc * 4 + jj
                for kc in range(8):
                    P.mm(psb[0][:, i * 48 + j:i * 48 + j + 1], sl[:, kc, jj * 128:(jj + 1) * 128],
                         cact[:, kc:kc + 1], kc == 0, kc == 7, [rs, R_cact], [R_ps[0]])
        if i == 0:
            cast_mlp(0)
    for i in range(1):
        P.tt("dve", modv[:, i, :], psb[0][:, i * 48:(i + 1) * 48],
             vecs[:, i * V_LAYER:i * V_LAYER + 48], ALU.add, [R_ps[0], R_vecs], [R_mod])
    P.fence([R_ada[0], R_ada[1]])
    A.pop()
    lv = A.tile([DEPTH, 6, 8], F32)
    R_lv = Res("lv")

    def derive(i):
        kind, j = i % 3, i // 3
        m = modv[:, i, :]
        vb = i * V_LAYER
        P.stt("dve", lv[:, i, 0, :], m[:, 8:16], 1.0, vecs[:, vb + 48:vb + 56], ALU.add, ALU.mult,
              [R_mod, R_vecs], [R_lv])
        P.copy("dve", lv[:, i, 1, :], m[:, 0:8], [R_mod], [R_lv])
        if kind == 0:
            P.tt("dve", lv[:, i, 2, :], m[:, 16:24], vecs[:, V_POOLSC + 8 * j:V_POOLSC + 8 * j + 8],
                 ALU.mult, [R_mod, R_vecs], [R_lv])
        else:
            P.copy("dve", lv[:, i, 2, :], m[:, 16:24], [R_mod], [R_lv])
        P.stt("dve", lv[:, i, 3, :], m[:, 32:40], 1.0, vecs[:, vb + 56:vb + 64], ALU.add, ALU.mult,
              [R_mod, R_vecs], [R_lv])
        P.copy("dve", lv[:, i, 4, :], m[:, 24:32], [R_mod], [R_lv])
        P.copy("dve", lv[:, i, 5, :], m[:, 40:48], [R_mod], [R_lv])

    cnt = {"sq": 0, "tmp": 0, "up": 0, "dn": 0}

    def rms_rstd(t0, T):
        sq, R_sq, rt, R_rt, rstd, R_rstd = L["sq"], L["R_sq"], L["rt"], L["R_rt"], L["rstd"], L["R_rstd"]
        for c in range(NCH):
            b = cnt["sq"] % 2
            cnt["sq"] += 1
            P.act(sq[b][:, 0:T], xT[:, c, t0:t0 + T], AF.Square, xres_c(c, t0, T), [R_sq[b]])
            P.mm(psb[0][:, 0:T], ones_m, sq[b][:, 0:T], c == 0, c == NCH - 1,
                 [R_sq[b], R_c], [R_ps[0]])
        P.act(rt[:, 0:T], psb[0][:, 0:T], AF.Sqrt, [R_ps[0], R_vecs], [R_rt], bias=eps_rms, scale=1.0)
        P.recip(rstd[:, 0:T], rt[:, 0:T], [R_rt], [R_rstd])

    def norm_mod(t0, T, Acol, Bcol, dst, dst_res, dst_off=0):
        rms_rstd(t0, T)
        tmp, R_tmp, rstd, R_rstd = L["tmp"], L["R_tmp"], L["rstd"], L["R_rstd"]
        for c in range(NCH):
            b = cnt["tmp"] % 2
            cnt["tmp"] += 1
            P.tt("dve", tmp[b][:, 0:T], xT[:, c, t0:t0 + T], rstd[:, 0:T], ALU.mult,
                 xres_c(c, t0, T) + [R_rstd], [R_tmp[b]])
            P.act(dst[:, c, dst_off:dst_off + T], tmp[b][:, 0:T], AF.Identity,
                  [R_tmp[b], R_lv], [dst_res[c]], bias=Bcol[:, c:c + 1], scale=Acol[:, c:c + 1])

    UP_BANKS = [1, 2, 3]
    DN_BANKS = [4, 5, 6, 7]

    def up_bank():
        if L.get("up_banks"):
            ub = L["up_banks"]
            b = ub[cnt["up"] % len(ub)]
            cnt["up"] += 1
            return b
        b = UP_BANKS[cnt["up"] % len(UP_BANKS)]
        cnt["up"] += 1
        return b

    def dn_bank():
        b = DN_BANKS[cnt["dn"] % len(DN_BANKS)]
        cnt["dn"] += 1
        return b

    def resid_add(t0, T, d, bank, Gcol):
        P.stt("dve", xT[:, d, t0:t0 + T], psb[bank][:, 0:T], Gcol[:, d:d + 1], xT[:, d, t0:t0 + T],
              ALU.mult, ALU.add, [R_ps[bank], R_lv] + xres_c(d, t0, T), xres_c(d, t0, T))

    def mlp_layer(i):
        T = 512
        begin_layer(T, 4)
        hT = [A.tile([8, T], BF16) for _ in range(2)]
        R_h = [[Res(f"h{b}_{c}") for c in range(8)] for b in range(2)]
        aT = [A.tile([4, T], BF16) for _ in range(2)]
        R_a = [[Res(f"a{b}_{f}") for f in range(4)] for b in range(2)]
        rl, R_rl = L["tmp"], L["R_tmp"]
        nrl = 0
        do_ada = False
        dnb = DN_BANKS
        ndn = 0
        if i + 1 < n_layers:
            nk = (i + 1) % 3
            if nk == 1:
                cast_sgu()
            elif nk == 2:
                cast_mla()
            else:
                cast_pool((i + 1) // 3)
            cast_mlp(i + 1)
        for t in range(S // T):
            t0 = t * T
            h, rh = hT[t % 2], R_h[t % 2]
            norm_mod(t0, T, lv[:, i, 3, :], lv[:, i, 4, :], h, rh)
            for q in range(8):
                w1, rw1 = wload(w1s[i][q], R_w1s[i])
                w1 = w1.rearrange("p (kc c) -> p kc c", kc=8)
                w2, rw2 = wload(w2s[i][q], R_w2s[i])
                w2 = w2.rearrange("p (fc d) -> p fc d", fc=4)
                a, ra = aT[q % 2], R_a[q % 2]
                for f in range(4):
                    bk = up_bank()
                    for kc in range(8):
                        P.mm(psb[bk][:, 0:T], w1[:, kc, f * 128:(f + 1) * 128], h[:, kc, :],
                             kc == 0, kc == 7, [rw1, rh[kc]], [R_ps[bk]])
                    r_, rr = rl[nrl % 2], R_rl[nrl % 2]
                    nrl += 1
                    P.act(r_, psb[bk][:, 0:T], AF.Relu, [R_ps[bk]], [rr])
                    P.tt("pool", a[:, f, :], r_, r_, ALU.mult, [rr], [ra[f]])
                for d in range(8):
                    bk = dnb[ndn % len(dnb)]
                    ndn += 1
                    for f in range(4):
                        P.mm(psb[bk][:, 0:T], w2[:, f, d * 128:(d + 1) * 128], a[:, f, :],
                             f == 0, f == 3, [rw2, ra[f]], [R_ps[bk]])
                    resid_add(t0, T, d, bk, lv[:, i, 5, :])
        extra = []
        end_layer([r for b in R_h for r in b] + [r for b in R_a for r in b] + extra)

    def pool_layer(i):
        j = i // 3
        T = 256
        H = 16
        begin_layer(T, 0)
        hp = A.tile([8, H + T], F32)
        R_hp = [Res(f"hp{c}") for c in range(8)]
        sA = A.tile([2, H + T], F32)
        sB = A.tile([2, H + T], F32)
        R_sA, R_sB = Res("sA"), Res("sB")
        pooled = A.tile([8, T], BF16)
        R_pl = Res("pooled")
        pw = A.tile([4, 2, 256], BF16)
        R_pw = Res("pw")
        t15 = A.tile([16], F32)
        R_t15 = Res("t15")
        P.dma("sp", pw, poolws[j].rearrange("p (g kc d) -> p g kc d", g=4, kc=2), [R_poolws[j]], [R_pw])
        P.memset("dve", hp[:, :, 0:H], 0.0, R_hp)
        P.memset("dve", sA, 0.0, [R_sA])
        P.memset("dve", sB, 0.0, [R_sB])
        ada_items = []
        if i == 0 and n_layers > 1:
            asl = [A.tile([8, 512], BF16) for _ in range(2)]
            R_asl = [Res("asl0"), Res("asl1")]
            for li in range(1, n_layers):
                for pc in range(12):
                    ada_items.append((li, pc))
        n_ada = len(ada_items)
        ada_k = 0
        L = H + T
        for t in range(S // T):
            t0 = t * T
            if t == 0 and _os.environ.get("DUMMY"):
                rms_rstd(int(_os.environ.get("DUMMY")) * T, T)
            norm_mod(t0, T, lv[:, i, 0, :], lv[:, i, 1, :], hp, R_hp, dst_off=H)
            for g in range(4):
                cs = slice(2 * g, 2 * g + 2)
                src, rsrc = hp[:, cs, :], R_hp[2 * g:2 * g + 2]
                bufs = [(sA, R_sA), (sB, R_sB)]
                sh = 1
                for step in range(g + 1):
                    dst, rdst = bufs[step % 2]
                    eng = "dve" if (step % 2 == 0 or _os.environ.get("NOPOOL")) else "pool"
                    P.tt(eng, dst[:, :, sh:L], src[:, :, sh:L], src[:, :, 0:L - sh], ALU.add,
                         rsrc, [rdst])
                    if sh > 1 or True:
                        pass
                    src, rsrc = dst, [rdst]
                    sh *= 2
                win = 2 ** (g + 1)
                P.stt("dve", pooled[:, cs, :], src[:, :, H:L], 1.0 / win, hp[:, cs, H:L],
                      ALU.mult, ALU.subtract, rsrc + R_hp[2 * g:2 * g + 2], [R_pl])
                if t == 0:
                    for cc in range(2):
                        c = 2 * g + cc
                        P.tt("dve", t15[:, 0:15], src[:, cc, H:H + 15],
                             vecs[:, V_INVC + 16 * g:V_INVC + 16 * g + 15], ALU.mult,
                             rsrc + [R_vecs], [R_t15])
                        P.tt("dve", pooled[:, c, 0:15], t15[:, 0:15], hp[:, c, H:H + 15], ALU.subtract,
                             [R_t15, R_hp[c]], [R_pl])
            want = ((t + 1) * n_ada + (S // T) - 1) // (S // T)
            while ada_k < min(want, n_ada):
                li, pc = ada_items[ada_k]
                sl, rs = asl[ada_k % 2], R_asl[ada_k % 2]
                ada_k += 1
                avl = ada_w_d[li].rearrange("(kc p) f -> p kc f", p=128)
                P.dma("pool", sl, avl[:, :, pc * 512:(pc + 1) * 512], (), [rs])
                for jj in range(4):
                    jcol = (li - 1) * 48 + pc * 4 + jj
                    for kc in range(8):
                        P.mm(psb[1][:, jcol:jcol + 1], sl[:, kc, jj * 128:(jj + 1) * 128],
                             cact[:, kc:kc + 1], kc == 0, kc == 7, [rs, R_cact], [R_ps[1]])
            if t + 1 < S // T:
                P.copy("dve", hp[:, :, 0:H], hp[:, :, T:T + H], R_hp, R_hp)
            for g in range(4):
                for dc in range(2):
                    bk = dn_bank()
                    for kc in range(2):
                        P.mm(psb[bk][:, 0:T], pw[:, g, kc, dc * 128:(dc + 1) * 128], pooled[:, 2 * g + kc, :],
                             kc == 0, kc == 1, [R_pw, R_pl], [R_ps[bk]])
                    resid_add(t0, T, 2 * g + dc, bk, lv[:, i, 2, :])
        extra = []
        if n_ada:
            for li in range(1, n_layers):
                P.tt("dve", modv[:, li, :], psb[1][:, (li - 1) * 48:li * 48],
                     vecs[:, li * V_LAYER:li * V_LAYER + 48], ALU.add, [R_ps[1], R_vecs], [R_mod])
            extra = R_asl
        end_layer(R_hp + [R_sA, R_sB, R_pl, R_pw, R_t15] + extra)


    def sgu_layer(i):
        T = 256
        begin_layer(T, 4)
        wsT = A.tile([8, 128], BF16)
        R_ws = Res("wsT")
        bterm = A.tile([8, 128], F32)
        R_bt = Res("bterm")
        hT = [A.tile([8, T], BF16) for _ in range(2)]
        R_h = [[Res(f"sh{b}_{c}") for c in range(8)] for b in range(2)]
        ub = [A.tile([T], F32) for _ in range(2)]
        R_u = [Res("u0"), Res("u1")]
        tm = [A.tile([T], F32) for _ in range(2)]
        R_tm = [Res("tm0"), Res("tm1")]
        vbuf = A.tile([1024], F32)
        R_v = Res("vbuf")
        vn = A.tile([2, 1024], BF16)
        R_vn = [Res("vn0"), Res("vn1")]
        gT = A.tile([8, T], BF16)
        R_g = [Res(f"gT{h}") for h in range(8)]
        st = A.tile([8], F32)
        R_st = Res("st")
        lng = vecs[:, V_LNG:V_LNG + 8]
        lnb = vecs[:, V_LNB:V_LNB + 8]
        vb3 = vbuf.rearrange("p (h t) -> p h t", h=8)
        P.dma("sp", vb3, sgu_wsT_d.rearrange("h s t -> s h t"), (), [R_v])
        for h in range(8):
            P.tt("dve", wsT[:, h, :], vb3[:, h, :], cmat[:, 128:256], ALU.mult, [R_v, R_cmat], [R_ws])
        bs_bc = bass.AP(sgu_bs_d.tensor, 0, [[0, 128], [1, 1024]])
        P.dma("sp", bterm.rearrange("p h t -> p (h t)"), bs_bc, (), [R_bt])
        for h in range(8):
            bk = up_bank()
            P.mm(psb[bk][:, 0:128], ones_1, wsT[:, h, :], True, True, [R_c, R_ws], [R_ps[bk]])
            P.stt("dve", bterm[:, h, :], psb[bk][:, 0:128], lnb[:, h:h + 1], bterm[:, h, :],
                  ALU.mult, ALU.add, [R_ps[bk], R_vecs, R_bt], [R_bt])
        nu = 0
        for t in range(S // T):
            t0 = t * T
            h_, rh = hT[t % 2], R_h[t % 2]
            norm_mod(t0, T, lv[:, i, 0, :], lv[:, i, 1, :], h_, rh)
            for ch in range(2):
                P.memset("dve", st[:, 0:4], 0.0, [R_st])
                for half in range(2):
                    wp, rwp = wload(sgu_wins[2 + half], R_sgu_wins)
                    wp = wp.rearrange("p (kc c) -> p kc c", kc=8)
                    bk = up_bank()
                    for kc in range(8):
                        P.mm(psb[bk][:, 0:512], h_[:, kc, ch * 128:(ch + 1) * 128], wp[:, kc, :],
                             kc == 0, kc == 7, [rh[kc], rwp], [R_ps[bk]])
                    P.act(vbuf[:, half * 512:(half + 1) * 512], psb[bk][:, 0:512], AF.Gelu,
                          [R_ps[bk], R_st], [R_v, R_st], accum_out=st[:, half:half + 1])
                P.tt("dve", st[:, 4:5], st[:, 0:1], st[:, 1:2], ALU.add, [R_st], [R_st])
                P.ts("dve", st[:, 4:5], st[:, 4:5], -1.0 / 1024, None, ALU.mult, None, [R_st], [R_st])
                P.act(vn[:, ch, :], vbuf, AF.Square, [R_v, R_st], [R_vn[ch], R_st],
                      bias=st[:, 4:5], scale=1.0, accum_out=st[:, 2:3])
                P.act(st[:, 5:6], st[:, 2:3], AF.Sqrt, [R_st, R_vecs], [R_st], bias=eps_ln, scale=1.0 / 1024)
                P.recip(st[:, 6:7], st[:, 5:6], [R_st], [R_st])
                P.ts("dve", vn[:, ch, :], vbuf, st[:, 4:5], st[:, 6:7], ALU.add, ALU.mult,
                     [R_v, R_st], [R_vn[ch]])
            for h in range(8):
                if h % 4 == 0:
                    up_, rup = wload(sgu_wins[h // 4], R_sgu_wins)
                    up_ = up_.rearrange("p (kc c) -> p kc c", kc=8)
                bk = up_bank()
                for kc in range(8):
                    P.mm(psb[bk][:, 0:T], up_[:, kc, (h % 4) * 128:(h % 4 + 1) * 128], h_[:, kc, :],
                         kc == 0, kc == 7, [rup, rh[kc]], [R_ps[bk]])
                u_, ru = ub[nu % 2], R_u[nu % 2]
                tm_, rtm = tm[nu % 2], R_tm[nu % 2]
                nu += 1
                P.act(u_, psb[bk][:, 0:T], AF.Gelu, [R_ps[bk]], [ru])
                bk2 = dn_bank()
                for ch in range(2):
                    P.mm(psb[bk2][:, ch * 128:(ch + 1) * 128], vn[:, ch, h * 128:(h + 1) * 128], wsT[:, h, :],
                         True, True, [R_vn[ch], R_ws], [R_ps[bk2]])
                for ch in range(2):
                    P.stt("dve", tm_[:, ch * 128:(ch + 1) * 128], psb[bk2][:, ch * 128:(ch + 1) * 128],
                          lng[:, h:h + 1], bterm[:, h, :], ALU.mult, ALU.add,
                          [R_ps[bk2], R_vecs, R_bt], [rtm])
                P.tt("dve", gT[:, h, :], tm_, u_, ALU.mult, [rtm, ru], [R_g[h]])
            for half in range(2):
                wo_, rwo = wload(sgu_wouts[half], R_sgu_wouts)
                wo_ = wo_.rearrange("p (kc c) -> p kc c", kc=8)
                for dd in range(4):
                    bk = dn_bank()
                    for kc in range(8):
                        P.mm(psb[bk][:, 0:T], wo_[:, kc, dd * 128:(dd + 1) * 128], gT[:, kc, :],
                             kc == 0, kc == 7, [rwo, R_g[kc]], [R_ps[bk]])
                    resid_add(t0, T, half * 4 + dd, bk, lv[:, i, 2, :])
        end_layer([R_ws, R_bt, R_v, R_st] + R_g + R_h[0] + R_h[1] + R_u + R_tm + R_vn)


    def bc_mid(a, n):
        ap = [list(x) for x in a.ap]
        return bass.AP(a.tensor, a.offset, [ap[0], [0, n]] + ap[1:])

    def mla_layer(i):
        T = 128
        NT = S // T
        begin_layer(T, 0)
        L["up_banks"] = [2, 3]
        tmp, R_tmp, rstd, R_rstd = L["tmp"], L["R_tmp"], L["rstd"], L["R_rstd"]
        SM = float(192 ** -0.5)
        wq_s = [A.tile([1536], BF16) for _ in range(2)]
        wk_s = [A.tile([1024], BF16) for _ in range(2)]
        wo_s = [A.tile([2048], BF16) for _ in range(2)]
        wdq_s = A.tile([8 * 448], BF16)
        R_wq = [Res("wq0"), Res("wq1")]
        R_wk = [Res("wk0"), Res("wk1")]
        R_wo = [Res("wo0"), Res("wo1")]
        R_wdq = Res("wdq")
        ckvT = A.tile([S], BF16)
        krT = A.tile([S], BF16)
        Vtok = A.tile([NT, 128], BF16)
        R_kv = [Res(f"kv{t}") for t in range(NT)]
        hT = A.tile([8, T], BF16)
        R_h = [Res(f"mh{c}") for c in range(8)]
        sqq = A.tile([384], BF16)
        R_sqq = Res("sqq")
        rq = A.tile([512], F32)
        R_rq = Res("rq")
        cqn = A.tile([2, T], BF16)
        R_cqn = Res("cqn")
        posf = A.tile([T], F32)
        thc = A.tile([T], F32)
        ths = A.tile([T], F32)
        t1 = A.tile([T], F32)
        R_rope = Res("rope")
        R_cs = Res("cossin")
        fa = A.tile([512], F32)
        fb = A.tile([512], F32)
        R_fa, R_fb = Res("fa"), Res("fb")
        qn_g = A.tile([4, T], BF16)
        qp_g = [A.tile([4, T], BF16) for _ in range(2)]
        qr_g = [A.tile([4, T], BF16) for _ in range(2)]
        on_g = A.tile([4, T], BF16)
        at_g = A.tile([4, T], BF16)
        R_qn, R_on, R_at = Res("qn"), Res("on"), Res("at")
        R_qp = [Res("qp0"), Res("qp1")]
        R_qr = [Res("qr0"), Res("qr1")]
        PT = [A.tile([512], BF16) for _ in range(2)]
        R_PT = [Res("PT0"), Res("PT1")]
        negm = A.tile([128], BF16)
        R_negm = Res("negm")
        P.copy("dve", negm, cmat[:, 256:384], [R_cmat], [R_negm])
        qng = vecs[:, V_QNG:V_QNG + 2]
        kvng = vecs[:, V_KVNG:V_KVNG + 1]
        fcos = vecs[:, V_FREQ:V_FREQ + 1]
        fsin = vecs[:, V_FREQ + 1:V_FREQ + 2]
        MAGIC = 12582912.0
        C1 = 6.28125
        C2 = float(2 * np.pi - 6.28125)
        PI = float(np.pi)
        S_BANKS = [4, 5]
        BO, BL = 6, 7
        st_ = {"ns": 0}

        def sin_of(dst, freq_col, shift):
            d = dst[0:64]
            n_ = t1[0:64]
            P.ts("dve", d, posf[0:64], freq_col[0:64], shift, ALU.mult, ALU.add, [R_rope, R_vecs], [R_cs])
            P.ts("dve", n_, d, 1.0 / (2 * PI), MAGIC, ALU.mult, ALU.add, [R_cs], [R_rope])
            P.ts("dve", n_, n_, -MAGIC, None, ALU.add, None, [R_rope], [R_rope])
            P.stt("dve", d, n_, -C1, d, ALU.mult, ALU.add, [R_rope, R_cs], [R_cs])
            P.stt("dve", d, n_, -C2, d, ALU.mult, ALU.add, [R_rope, R_cs], [R_cs])
            P.ts("dve", d, d, -PI, PI, ALU.max, ALU.min, [R_cs], [R_cs])
            P.act(d, d, AF.Sin, [R_cs], [R_cs])

        def rope(dst_bf, src_ps, R_src, nh, R_dst):
            n = nh * T
            a_, b_ = fa[0:64, 0:n], fb[:, 0:n]
            if nh == 1:
                cosb, sinb_lo, sinb_hi = thc[0:64], ths[0:32], ths[32:64]
                s3 = lambda ap: ap
            else:
                cosb = bc_mid(thc[0:64], nh)
                sinb_lo, sinb_hi = bc_mid(ths[0:32], nh), bc_mid(ths[32:64], nh)
                s3 = lambda ap: ap.rearrange("p (h t) -> p h t", h=nh)
            P.tt("dve", s3(a_), s3(src_ps[0:64, 0:n]), cosb, ALU.mult, [R_src, R_cs], [R_fa])
            P.tt("dve", s3(b_[0:32]), s3(src_ps[32:64, 0:n]), sinb_hi, ALU.mult, [R_src, R_cs], [R_fb])
            P.tt("dve", s3(b_[32:64]), s3(src_ps[0:32, 0:n]), sinb_lo, ALU.mult, [R_src, R_cs], [R_fb])
            P.tt("dve", dst_bf, a_, b_[0:64], ALU.add, [R_fa, R_fb], [R_dst])

        def A_norm(it):
            norm_mod(it * T, T, lv[:, i, 0, :], lv[:, i, 1, :], hT, R_h)

        actx = {}

        def A_lat(it):
            t0 = it * T
            P.dma("sp", wdq_s, mla_wdqs, [R_mla_wdqs], [R_wdq])
            wdq = wdq_s.rearrange("p (kc c) -> p kc c", kc=8)
            bx = 1
            actx[it] = bx
            for fc in range(4):
                M = 128 if fc < 3 else 64
                for kc in range(8):
                    P.mm(psb[bx][0:M, fc * 128:(fc + 1) * 128], wdq[:, kc, fc * 128:fc * 128 + M], hT[:, kc, :],
                         kc == 0, kc == 7, [R_wdq, R_h[kc]], [R_ps[bx]])
            pos_bc = bass.AP(pos_d.tensor, t0, [[0, 64], [1, T]])
            P.dma("sp", posf[0:64].bitcast(I32), pos_bc, (), [R_rope])
            P.copy("dve", posf[0:64], posf[0:64].bitcast(I32), [R_rope], [R_rope])
            sin_of(thc, fcos, PI / 2)
            sin_of(ths, fsin, 0.0)
            P.act(sqq[:, 0:384], psb[bx][:, 0:384], AF.Square, [R_ps[bx]], [R_sqq])
            P.mm(psb[0][:, 0:128], ones_q, sqq[:, 0:128], True, False, [R_c, R_sqq], [R_ps[0]])
            P.mm(psb[0][:, 0:128], ones_q, sqq[:, 128:256], False, True, [R_c, R_sqq], [R_ps[0]])
            P.mm(psb[0][:, 128:256], ones_kv, sqq[:, 256:384], True, True, [R_c, R_sqq], [R_ps[0]])
            P.act(rq[:, 0:256], psb[0][:, 0:256], AF.Sqrt, [R_ps[0], R_vecs], [R_rq], bias=eps_rms, scale=1.0)
            P.recip(rq[:, 256:512], rq[:, 0:256], [R_rq], [R_rq])

        def A_fin(it):
            t0 = it * T
            cols = slice(t0, t0 + T)
            bx = actx.pop(it)
            for kc in range(2):
                P.stt("dve", cqn[:, kc, :], psb[bx][:, kc * 128:(kc + 1) * 128], qng[:, kc:kc + 1],
                      rq[:, 256:384], ALU.mult, ALU.mult, [R_ps[bx], R_vecs, R_rq], [R_cqn])
            P.stt("dve", ckvT[:, cols], psb[bx][:, 256:384], kvng, rq[:, 384:512], ALU.mult, ALU.mult,
                  [R_ps[bx], R_vecs, R_rq], [R_kv[it]])
            rope(krT[0:64, cols], psb[bx][:, 384:512], R_ps[bx], 1, R_kv[it])
            bv = up_bank()
            P.mm(psb[bv][:, 0:128], ckvT[:, cols], ident, True, True, [R_kv[it], R_c], [R_ps[bv]])
            P.act(Vtok[:, it, :], psb[bv][:, 0:128], AF.Copy, [R_ps[bv]], [R_kv[it]])

        def P1(u):
            it, g = u
            k = (it * 4 + g) % 2
            P.dma("sp", wq_s[k], mla_grps[g][:, 0:1536], [R_mla_grps], [R_wq[k]])
            P.dma("sp", wk_s[k], mla_grps[g][:, 1536:2560], [R_mla_grps], [R_wk[k]])
            wuq_g = wq_s[k].rearrange("p (kc c) -> p kc c", kc=2)
            bq = up_bank()
            for hh in range(4):
                for kc in range(2):
                    P.mm(psb[bq][:, hh * 128:(hh + 1) * 128], wuq_g[:, kc, hh * 192:hh * 192 + 128], cqn[:, kc, :],
                         kc == 0, kc == 1, [R_wq[k], R_cqn], [R_ps[bq]])
            P.act(qn_g.rearrange("p h t -> p (h t)"), psb[bq][:, 0:512], AF.Copy, [R_ps[bq]], [R_qn])
            br = up_bank()
            for hh in range(4):
                for kc in range(2):
                    P.mm(psb[br][0:64, hh * 128:(hh + 1) * 128], wuq_g[:, kc, hh * 192 + 128:hh * 192 + 192],
                         cqn[:, kc, :], kc == 0, kc == 1, [R_wq[k], R_cqn], [R_ps[br]])
            rope(qr_g[k][0:64].rearrange("p h t -> p (h t)"), psb[br], R_ps[br], 4, R_qr[k])

        def P2(u):
            it, g = u
            k = (it * 4 + g) % 2
            wukT_g = wk_s[k][:, 0:512].rearrange("p (h c) -> p h c", h=4)
            ba = up_bank()
            for hh in range(4):
                P.mm(psb[ba][:, hh * 128:(hh + 1) * 128], wukT_g[:, hh, :], qn_g[:, hh, :], True, True,
                     [R_wk[k], R_qn], [R_ps[ba]])
            P.act(qp_g[k].rearrange("p h t -> p (h t)"), psb[ba][:, 0:512], AF.Copy, [R_ps[ba]], [R_qp[k]])

        def QK(u, j):
            it, g = u
            k = (it * 4 + g) % 2
            qp2 = qp_g[k].rearrange("p h t -> p (h t)")
            qr2 = qr_g[k][0:64].rearrange("p h t -> p (h t)")
            ns = st_["ns"]
            st_["ns"] += 1
            bs_ = S_BANKS[ns % 2]
            pt_, rpt = PT[ns % 2], R_PT[ns % 2]
            kc_ = slice(j * 128, (j + 1) * 128)
            P.mm(psb[bs_][:, 0:512], ckvT[:, kc_], qp2, True, False, [R_kv[j], R_qp[k]], [R_ps[bs_]])
            P.mm(psb[bs_][:, 0:512], krT[0:64, kc_], qr2, False, j < it, [R_kv[j], R_qr[k]], [R_ps[bs_]])
            if j == it:
                for hh in range(4):
                    P.mm(psb[bs_][:, hh * 128:(hh + 1) * 128], ident, negm, False, hh == 3,
                         [R_c, R_negm], [R_ps[bs_]])
            P.act(pt_, psb[bs_][:, 0:512], AF.Exp, [R_ps[bs_]], [rpt], scale=SM)
            return pt_, rpt

        def PV(u, j, pt_, rpt):
            it, g = u
            P.mm(psb[BO][:, 0:512], Vtok[:, j, :], pt_, j == 0, j == it, [R_kv[j], rpt], [R_ps[BO]])
            P.mm(psb[BL][:, 0:512], ones_1, pt_, j == 0, j == it, [R_c, rpt], [R_ps[BL]])

        def E1(u):
            P.act(on_g.rearrange("p h t -> p (h t)"), psb[BO][:, 0:512], AF.Copy, [R_ps[BO]], [R_on])
            P.recip(fa, psb[BL][:, 0:512], [R_ps[BL]], [R_fa])

        def E2(u):
            it, g = u
            k = (it * 4 + g) % 2
            wuv_g = wk_s[k][:, 512:1024].rearrange("p (h c) -> p h c", h=4)
            bu = up_bank()
            for hh in range(4):
                P.mm(psb[bu][:, hh * 128:(hh + 1) * 128], wuv_g[:, hh, :], on_g[:, hh, :], True, True,
                     [R_wk[k], R_on], [R_ps[bu]])
            P.tt("dve", at_g.rearrange("p h t -> p (h t)"), psb[bu][:, 0:512], fa, ALU.mult,
                 [R_ps[bu], R_fa], [R_at])

        def E3(u):
            it, g = u
            t0 = it * T
            cols = slice(t0, t0 + T)
            for half in range(2):
                P.dma("sp", wo_s[half], mla_wos[g].rearrange("p (h d) -> p h d", h=4)[:, :, half * 512:(half + 1) * 512],
                      [R_mla_wos], [R_wo[half]])
                wo_ = wo_s[half].rearrange("p (h d) -> p h d", h=4)
                bo_ = up_bank()
                for dd in range(4):
                    for hh in range(4):
                        P.mm(psb[bo_][:, dd * 128:(dd + 1) * 128], wo_[:, hh, dd * 128:(dd + 1) * 128],
                             at_g[:, hh, :], hh == 0, hh == 3, [R_wo[half], R_at], [R_ps[bo_]])
                for dd in range(4):
                    d = half * 4 + dd
                    P.stt("dve", xT[:, d, cols], psb[bo_][:, dd * 128:(dd + 1) * 128],
                          lv[:, i, 2, d:d + 1], xT[:, d, cols], ALU.mult, ALU.add,
                          [R_ps[bo_], R_lv] + xres_c(d, t0, T), xres_c(d, t0, T))

        units = [(it, g) for it in range(NT) for g in range(4)]
        A_norm(0)
        A_lat(0)
        A_fin(0)
        P1(units[0])
        P2(units[0])
        pend = None
        for idx, u in enumerate(units):
            it, g = u
            nxt = units[idx + 1] if idx + 1 < len(units) else None
            todo = []
            if pend is not None:
                todo.append((0, lambda p=pend: E2(p)))
            if nxt is not None:
                todo.append((0, lambda n=nxt: P1(n)))
                todo.append((1, lambda n=nxt: P2(n)))
            if pend is not None:
                todo.append((2, lambda p=pend: E3(p)))
            if it + 1 < NT:
                if g == 1:
                    todo.append((0, lambda n=it + 1: A_norm(n)))
                elif g == 2:
                    todo.append((0, lambda n=it + 1: A_lat(n)))
                    todo.append((3, lambda n=it + 1: A_fin(n)))
            prev = None
            for j in range(it + 1):
                cur = QK(u, j)
                if prev is not None:
                    PV(u, j - 1, *prev)
                prev = cur
                rest = []
                for (bi, fn) in todo:
                    if bi <= j:
                        fn()
                    else:
                        rest.append((bi, fn))
                todo = rest
            PV(u, it, *prev)
            for (bi, fn) in todo:
                fn()
            E1(u)
            pend = u
        E2(pend)
        E3(pend)
        end_layer(R_kv + R_h + [R_sqq, R_rq, R_cqn, R_rope, R_cs, R_fa, R_fb, R_qn, R_on, R_at, R_negm,
                                R_wdq] + R_PT + R_qp + R_qr + R_wq + R_wk + R_wo)

    def final_layer():
        T = 512
        begin_layer(T, 0)
        tmp, R_tmp, rstd, R_rstd = L["tmp"], L["R_tmp"], L["rstd"], L["R_rstd"]
        stg = [A.tile([T], F32) for _ in range(4)]
        R_stg = [Res(f"stg{i}") for i in range(4)]
        ov = out_d.rearrange("(c p) t -> p c t", p=128)
        R_out = Res("out")
        n = 0
        for t in range(S // T):
            t0 = t * T
            rms_rstd(t0, T)
            for c in range(NCH):
                b = cnt["tmp"] % 2
                cnt["tmp"] += 1
                P.tt("dve", tmp[b][:, 0:T], xT[:, c, t0:t0 + T], rstd[:, 0:T], ALU.mult,
                     xres_c(c, t0, T) + [R_rstd], [R_tmp[b]])
                s_, rs_ = stg[n % 4], R_stg[n % 4]
                n += 1
                P.act(s_, tmp[b][:, 0:T], AF.Identity, [R_tmp[b], R_vecs], [rs_],
                      scale=vecs[:, V_FINALG + c:V_FINALG + c + 1])
                P.dma("sp", ov[:, c, t0:t0 + T], s_, [rs_], [R_out], group=True)
        P.op("sp", lambda e: e.nop(), [R_out], [])
        end_layer(R_stg)

    def raw_store():
        ov = out_d.rearrange("(c p) t -> p c t", p=128)
        R_out = Res("out")
        for t in range(8):
            P.dma("sp", ov[:, :, t * 512:(t + 1) * 512], xT[:, :, t * 512:(t + 1) * 512],
                  xres(t * 512, 512), [R_out], group=True)
        P.op("sp", lambda e: e.nop(), [R_out], [])

    for i in range(n_layers):
        derive(i)
        kind = i % 3
        if kind == 0:
            pool_layer(i)
        elif kind == 1:
            sgu_layer(i)
        else:
            mla_layer(i)
        if not (_os.environ.get("SKIP_MLP") and i == n_layers - 1):
            mlp_layer(i)
    if final:
        final_layer()
    else:
        raw_store()

    nwait = P.emit()
    info = {"ops": len(P.ops), "waits": nwait, "sems": P.nsem, "arena_peak_words": A.peak, "log": P.log}
    return nc, stack, info


def _pcol(v, ncol):
    return np.ascontiguousarray(np.asarray(v, np.float32).reshape(ncol, 128).T)


def prep_shared(inp):
    f = lambda a: np.ascontiguousarray(np.asarray(a, np.float32))
    sh = {}
    sh["ada_w"] = f(inp["ada_w"])
    sh["mlp_w1"] = f(inp["mlp_w1"])
    sh["mlp_w2"] = f(inp["mlp_w2"])
    sh["pool_w"] = f(inp["pool_w"])
    sh["sgu_w_in"] = f(inp["sgu_w_in"][0])
    sh["sgu_w_out"] = f(inp["sgu_w_out"][0])
    sh["sgu_wsT"] = f(np.transpose(np.asarray(inp["sgu_w_s"][0]), (0, 2, 1)))
    sh["sgu_ln"] = f(np.stack([np.asarray(inp["sgu_ln_g"][0]), np.asarray(inp["sgu_ln_b"][0])]))
    sh["sgu_bs"] = f(np.asarray(inp["sgu_b_s"][0]).reshape(1, 1024))
    sh["mla_wdq"] = f(inp["mla_w_dq_dkv"][0])
    sh["mla_wuq"] = f(inp["mla_w_uq"][0])
    wukv = np.asarray(inp["mla_w_ukv"][0], np.float32).reshape(128, 16, 256)
    sh["mla_wukT"] = f(np.transpose(wukv[:, :, :128], (1, 2, 0)))
    sh["mla_wuv"] = f(wukv[:, :, 128:])
    sh["mla_wo"] = f(inp["mla_w_o"][0])
    idx = np.arange(128)
    cm = np.zeros((128, 384), np.float32)
    cm[:, 0:128] = np.eye(128, dtype=np.float32)
    cm[:, 128:256] = (idx[:, None] <= idx[None, :]).astype(np.float32)
    cm[:, 256:384] = np.where(idx[:, None] <= idx[None, :], 0.0, NEG).astype(np.float32)
    sh["cmat"] = cm
    return sh


def prep_vecs(inp, b):
    v = np.zeros((128, NV), np.float32)
    for i in range(DEPTH):
        o = i * V_LAYER
        v[:, o:o + 48] = _pcol(inp["ada_b"][i], 48)
        v[:, o + 48:o + 56] = _pcol(inp["norm_mix_g"][i], 8)
        v[:, o + 56:o + 64] = _pcol(inp["norm_mlp_g"][i], 8)
    for j in range(2):
        v[:, V_POOLSC + 8 * j:V_POOLSC + 8 * j + 8] = _pcol(inp["pool_scale"][j], 8)
    v[:, V_FINALG:V_FINALG + 8] = _pcol(inp["final_g"], 8)
    v[:, V_QNG:V_QNG + 2] = _pcol(inp["mla_q_norm_g"][0], 2)
    v[:, V_KVNG:V_KVNG + 1] = _pcol(inp["mla_kv_norm_g"][0], 1)
    v[:, V_C:V_C + 8] = _pcol(inp["c"][b], 8)
    fr = (10000.0 ** (-np.arange(0, 64, 2, dtype=np.float32) / 64)).astype(np.float32)
    p = np.arange(128)
    v[:, V_FREQ] = fr[p % 32]
    v[:, V_FREQ + 1] = np.where((p % 64) < 32, fr[p % 32], -fr[p % 32])
    for g, win in enumerate((2, 4, 8, 16)):
        v[:, V_INVC + 16 * g:V_INVC + 16 * g + 16] = (1.0 / np.minimum(np.arange(16) + 1.0, float(win))).astype(np.float32)[None, :]
    v[:, V_EPS] = RMS_EPS
    v[:, V_EPS + 1] = LN_EPS
    v[:, V_HALFPI] = np.float32(np.pi / 2)
    v[:, V_LNG:V_LNG + 8] = _pcol(inp["sgu_ln_g"][0], 8)
    v[:, V_LNB:V_LNB + 8] = _pcol(inp["sgu_ln_b"][0], 8)
    return v


def make_in_maps(inp, cores):
    sh = prep_shared(inp)
    maps = []
    for b in cores:
        m = dict(sh)
        m["xT"] = np.ascontiguousarray(np.asarray(inp["x"][b], np.float32).T)
        m["vecs"] = prep_vecs(inp, b)
        m["pos"] = np.ascontiguousarray(np.asarray(inp["positions"][b], np.int32).reshape(1, S))
        maps.append(m)
    return maps


_CACHE = {}


def kernel(**inputs):
    if "nc" not in _CACHE:
        _CACHE["nc"] = build()
    nc, stack, info = _CACHE["nc"]
    in_maps = make_in_maps(inputs, list(range(8)))
    res = run_bass_kernel_spmd(nc, in_maps, core_ids=list(range(8)))
    out = np.stack([np.ascontiguousarray(res.results[b]["outT"].T) for b in range(8)], axis=0)
    return out.astype(np.float32)
```
